# Optimizing a Trainium2 kernel written in Bass

```python
import jax, jax.numpy as jnp
from jax import lax
import numpy as np

D_MODEL = 1024
BATCH = 8
SEQ = 4096
DEPTH = 1

GRID_W = 64
CTX_LEN = 256
N_HEADS_A = 8
HEAD_DIM_A = 128
WIDTH_A = N_HEADS_A * HEAD_DIM_A
QKV_CONV = 3
CHUNK = 64
WIDTH_B = 1024
N_GROUPS_B = 8
CONV_B = 3
N_DIR = 2
SPLIT_SIZES = (3 * WIDTH_A, WIDTH_A, 2 * N_DIR * N_HEADS_A, WIDTH_B, WIDTH_B, WIDTH_B, WIDTH_B, D_MODEL, D_MODEL)
IN_COLS = sum(SPLIT_SIZES)
DN_ALPHA = (2.0 * DEPTH) ** 0.25
DN_BETA = (8.0 * DEPTH) ** -0.25
LN_EPS = 1e-5
RMS_EPS = 1e-6
L2_EPS = 1e-6

kernel_name = "hybrid_gdn_shortconv_dit_block"


def _layernorm(x, gain=None, bias=None):
    xf = x.astype(jnp.float32)
    mu = jnp.mean(xf, axis=-1, keepdims=True)
    var = jnp.mean(jnp.square(xf - mu), axis=-1, keepdims=True)
    y = (xf - mu) * lax.rsqrt(var + LN_EPS)
    if gain is not None:
        y = y * gain.astype(jnp.float32) + bias.astype(jnp.float32)
    return y.astype(x.dtype)


def _conv_seq(u, w):
    k = w.shape[0]
    r = k // 2
    length = u.shape[1]
    up = jnp.pad(u, ((0, 0), (r, r), (0, 0)))
    out = up[:, 0:length] * w[0]
    for i in range(1, k):
        out = out + up[:, i:i + length] * w[i]
    return out


def _conv_latent(u, w):
    b, length, ch = u.shape
    rows = length // GRID_W
    y = _conv_seq(u.reshape(b * rows, GRID_W, ch), w)
    return y.reshape(b, length, ch)


def _split_cols(p):
    idx = [int(i) for i in np.cumsum(SPLIT_SIZES)[:-1]]
    return jnp.split(p, idx, axis=-1)


def _heads(t):
    b, length, _ = t.shape
    return t.reshape(b, length, N_HEADS_A, HEAD_DIM_A).transpose(0, 2, 1, 3)


def _l2norm(t):
    tf = t.astype(jnp.float32)
    return tf * lax.rsqrt(jnp.sum(tf * tf, axis=-1, keepdims=True) + L2_EPS)


def _gdn_inputs(qkv, ab, conv_w, conv_fn):
    qkv = jax.nn.silu(conv_fn(qkv, conv_w))
    q, k, v = jnp.split(qkv, 3, axis=-1)
    q = _l2norm(_heads(q)) * (HEAD_DIM_A ** -0.5)
    k = _l2norm(_heads(k))
    v = _heads(v).astype(jnp.float32)
    ab = jnp.transpose(ab.astype(jnp.float32), (0, 2, 1))
    a_logit, b_logit = jnp.split(ab, 2, axis=1)
    return q, k, v, a_logit, b_logit


def _gdn_chunked(q, k, v, g, beta, s0):
    b, h, length, dk = q.shape
    dv = v.shape[-1]
    n = length // CHUNK
    q = q.reshape(b, h, n, CHUNK, dk)
    k = k.reshape(b, h, n, CHUNK, dk)
    v = v.reshape(b, h, n, CHUNK, dv)
    g_cum = jnp.cumsum(g.reshape(b, h, n, CHUNK), axis=-1)
    beta = beta.reshape(b, h, n, CHUNK)
    tri_incl = jnp.tril(jnp.ones((CHUNK, CHUNK), dtype=bool))
    tri_strict = jnp.tril(jnp.ones((CHUNK, CHUNK), dtype=bool), k=-1)
    diff = g_cum[..., :, None] - g_cum[..., None, :]
    decay = jnp.exp(jnp.where(tri_incl, diff, -jnp.inf))
    k_beta = k * beta[..., None]
    v_beta = v * beta[..., None]
    m = jnp.where(tri_strict, jnp.einsum('bhncd,bhnsd->bhncs', k_beta, k) * decay, 0.0)
    eye = jnp.eye(CHUNK, dtype=jnp.float32)
    t_inv = lax.linalg.triangular_solve(eye + m, jnp.broadcast_to(eye, m.shape), left_side=True, lower=True)
    u = jnp.einsum('bhncs,bhnsv->bhncv', t_inv, v_beta)
    w = jnp.einsum('bhncs,bhnsd->bhncd', t_inv, k_beta * jnp.exp(g_cum)[..., None])
    qk = jnp.einsum('bhncd,bhnsd->bhncs', q, k) * decay
    q_dec = q * jnp.exp(g_cum)[..., None]
    k_dec = k * jnp.exp(g_cum[..., -1:] - g_cum)[..., None]
    chunk_decay = jnp.exp(g_cum[..., -1])

    def step(s, xs):
        qk_i, q_dec_i, k_dec_i, u_i, w_i, d_i = xs
        v_new = u_i - jnp.einsum('bhck,bhkv->bhcv', w_i, s)
        o = jnp.einsum('bhck,bhkv->bhcv', q_dec_i, s) + jnp.einsum('bhcs,bhsv->bhcv', qk_i, v_new)
        s = s * d_i[..., None, None] + jnp.einsum('bhck,bhcv->bhkv', k_dec_i, v_new)
        return s, o

    xs = tuple(jnp.moveaxis(t, 2, 0) for t in (qk, q_dec, k_dec, u, w, chunk_decay))
    s_final, o = lax.scan(step, s0, xs)
    o = jnp.moveaxis(o, 0, 2).reshape(b, h, length, dv)
    return s_final, o


def _direction_inputs(inp, a_rate_d, dtb_d, d, reverse):
    q, k, v, a_logit, b_logit = inp
    hs = slice(d * N_HEADS_A, (d + 1) * N_HEADS_A)
    g = -a_rate_d[:, None] * jax.nn.softplus(a_logit[:, hs] + dtb_d[:, None])
    beta = jax.nn.sigmoid(b_logit[:, hs])
    seqs = (q, k, v, g, beta)
    if reverse:
        seqs = tuple(jnp.flip(t, axis=2) for t in seqs)
    return seqs


def _gdn_bidir(in_c, in_x, a_log, dt_bias):
    a_rate = jnp.exp(a_log.astype(jnp.float32))
    dtb = dt_bias.astype(jnp.float32)
    b = in_x[0].shape[0]
    outs_c, outs_x = [], []
    for d in range(N_DIR):
        reverse = d == 1
        s0 = jnp.zeros((b, N_HEADS_A, HEAD_DIM_A, HEAD_DIM_A), jnp.float32)
        s_c, o_c = _gdn_chunked(*_direction_inputs(in_c, a_rate[d], dtb[d], d, reverse), s0)
        _, o_x = _gdn_chunked(*_direction_inputs(in_x, a_rate[d], dtb[d], d, reverse), s_c)
        if reverse:
            o_c = jnp.flip(o_c, axis=2)
            o_x = jnp.flip(o_x, axis=2)
        outs_c.append(o_c)
        outs_x.append(o_x)
    return outs_c[0] + outs_c[1], outs_x[0] + outs_x[1]


def _gated_rmsnorm(o, z, w):
    o = jnp.transpose(o, (0, 2, 1, 3))
    o = o * lax.rsqrt(jnp.mean(o * o, axis=-1, keepdims=True) + RMS_EPS) * w.astype(jnp.float32)
    y = o * jax.nn.silu(z.reshape(o.shape).astype(jnp.float32))
    return y.reshape(o.shape[0], o.shape[1], WIDTH_A).astype(z.dtype)


def _branch_merge(o_a, p, conv_fn, o_norm_w, conv_b_w, conv_b_b, w_a, w_b, w_out):
    _, z_a, _, x_in, b_gate, c_gate, z_b, g_a, g_b = p
    y_a = _gated_rmsnorm(o_a, z_a, o_norm_w)
    y_b = b_gate * (conv_fn(c_gate * x_in, conv_b_w) + conv_b_b) * jax.nn.silu(z_b)
    merged = jax.nn.sigmoid(g_a) * (y_a @ w_a) + jax.nn.sigmoid(g_b) * (y_b @ w_b)
    return merged @ w_out


def setup_inputs(seed: int = 0) -> dict:
    key = jax.random.key(seed)
    ks = jax.random.split(key, 20)
    f32 = jnp.float32
    nrm = lambda k, shape, s: jax.random.normal(k, shape, f32) * s
    dt = jnp.exp(jax.random.uniform(ks[9], (DEPTH, N_DIR, N_HEADS_A), f32, np.log(1e-3), np.log(1e-1)))
    return {
        "x": nrm(ks[0], (BATCH, SEQ, D_MODEL), 1.0),
        "c": nrm(ks[1], (BATCH, D_MODEL), 1.0),
        "ctx": nrm(ks[2], (BATCH, CTX_LEN, D_MODEL), 1.0),
        "c_ctx": nrm(ks[3], (D_MODEL,), 1.0),
        "w_mod": nrm(ks[4], (DEPTH, D_MODEL, 3 * D_MODEL), 0.5 * D_MODEL ** -0.5),
        "b_mod": nrm(ks[5], (DEPTH, 3 * D_MODEL), 0.01),
        "w_in": nrm(ks[6], (DEPTH, D_MODEL, IN_COLS), D_MODEL ** -0.5),
        "b_in": nrm(ks[7], (DEPTH, IN_COLS), 0.01),
        "conv_qkv_w": nrm(ks[8], (DEPTH, QKV_CONV, 3 * WIDTH_A), QKV_CONV ** -0.5),
        "a_log": jnp.log(jax.random.uniform(ks[10], (DEPTH, N_DIR, N_HEADS_A), f32, 1.0, 16.0)),
        "dt_bias": dt + jnp.log(-jnp.expm1(-dt)),
        "o_norm_w": 1.0 + nrm(ks[11], (DEPTH, HEAD_DIM_A), 0.01),
        "conv_b_w": nrm(ks[12], (DEPTH, CONV_B, WIDTH_B), CONV_B ** -0.5),
        "conv_b_b": nrm(ks[13], (DEPTH, WIDTH_B), 0.01),
        "w_a": nrm(ks[14], (DEPTH, WIDTH_A, D_MODEL), DN_BETA * WIDTH_A ** -0.5),
        "w_b": nrm(ks[15], (DEPTH, WIDTH_B, D_MODEL), DN_BETA * WIDTH_B ** -0.5),
        "w_out": nrm(ks[16], (DEPTH, D_MODEL, D_MODEL), DN_BETA * D_MODEL ** -0.5),
        "ln_g": 1.0 + nrm(ks[17], (DEPTH, D_MODEL), 0.01),
        "ln_b": nrm(ks[18], (DEPTH, D_MODEL), 0.01),
    }


def reference(x, c, ctx, c_ctx, w_mod, b_mod, w_in, b_in, conv_qkv_w, a_log, dt_bias, o_norm_w,
              conv_b_w, conv_b_b, w_a, w_b, w_out, ln_g, ln_b):
    for l in range(DEPTH):
        mod_x = jax.nn.silu(c) @ w_mod[l] + b_mod[l]
        mod_c = jax.nn.silu(c_ctx) @ w_mod[l] + b_mod[l]
        shift_x, scale_x, gate_x = jnp.split(mod_x[:, None, :], 3, axis=-1)
        shift_c, scale_c, gate_c = jnp.split(mod_c, 3, axis=-1)
        hx = _layernorm(x) * (1.0 + scale_x) + shift_x
        hc = _layernorm(ctx) * (1.0 + scale_c) + shift_c
        px = _split_cols(hx @ w_in[l] + b_in[l])
        pc = _split_cols(hc @ w_in[l] + b_in[l])
        in_x = _gdn_inputs(px[0], px[2], conv_qkv_w[l], _conv_latent)
        in_c = _gdn_inputs(pc[0], pc[2], conv_qkv_w[l], _conv_seq)
        o_a_c, o_a_x = _gdn_bidir(in_c, in_x, a_log[l], dt_bias[l])
        y_x = _branch_merge(o_a_x, px, _conv_latent, o_norm_w[l], conv_b_w[l], conv_b_b[l], w_a[l], w_b[l], w_out[l])
        if l < DEPTH - 1:
            y_c = _branch_merge(o_a_c, pc, _conv_seq, o_norm_w[l], conv_b_w[l], conv_b_b[l], w_a[l], w_b[l], w_out[l])
            ctx = _layernorm(DN_ALPHA * ctx + gate_c * y_c, ln_g[l], ln_b[l])
        x = _layernorm(DN_ALPHA * x + gate_x * y_x, ln_g[l], ln_b[l])
    return x
```

```python
import numpy as np
import concourse.bass as bass
import concourse.mybir as mybir
from concourse.bass_utils import run_bass_kernel_spmd

F32 = mybir.dt.float32
F32R = mybir.dt.float32r
BF16 = mybir.dt.bfloat16
ALU = mybir.AluOpType
AF = mybir.ActivationFunctionType

D = 1024
SEQ = 4096
CTX = 256
NTOK = SEQ + CTX
H = 8
DH = 128
CH = 64
NCH = NTOK // CH
IN_COLS = 10272
DN_ALPHA = 2.0 ** 0.25
LN_EPS = 1e-5
RMS_EPS = 1e-6
L2_EPS = 1e-6
NEG = -1.0e30

SEM_LIM = 30000
N_DMA_SEMS = 24


class Op:
    __slots__ = ("eng", "fn", "reads", "writes", "kind", "idx", "lidx", "signal",
                 "waits", "cnt", "dma_i", "is_bar", "cost", "grp", "done")

    def __init__(self, eng, fn, reads, writes, kind):
        self.eng = eng
        self.fn = fn
        self.reads = reads
        self.writes = writes
        self.kind = kind
        self.signal = False
        self.waits = []
        self.cnt = None
        self.dma_i = None
        self.is_bar = False
        self.cost = None
        self.grp = 0
        self.done = False


class Sched:
    ENGS = ("pe", "act", "dve", "pool", "sp")

    def __init__(self, nc):
        self.nc = nc
        self.ops = []

    def add(self, eng, fn, r=(), w=(), kind="c", c=None):
        op = Op(eng, fn, tuple(r), tuple(w), kind)
        op.cost = c
        op.idx = len(self.ops)
        if eng == "pe":
            if self.ops and self.ops[-1].eng == "pe":
                op.grp = self.ops[-1].grp
            else:
                self._ngrp = getattr(self, "_ngrp", 0) + 1
                op.grp = self._ngrp
        self.ops.append(op)
        return op

    def dma(self, eng, fn, r=(), w=()):
        return self.add(eng, fn, r, w, kind="d")

    def barrier(self, fn):
        op = self.add("pool", fn, (), ("__bar",))
        op.is_bar = True
        return op

    DEF_COST = {"pe": 70.0, "act": 600.0, "dve": 650.0, "pool": 1250.0, "sp": 60.0}

    def finalize(self, reorder=True):
        import heapq
        nc = self.nc
        ops = self.ops
        last_w = {}
        readers = {}
        deps_of = [None] * len(ops)
        seg_of = [0] * len(ops)
        seg = 0
        for op in ops:
            if op.is_bar:
                seg += 1
                seg_of[op.idx] = seg
                seg += 1
                deps_of[op.idx] = {}
                last_w = {}
                readers = {}
                continue
            seg_of[op.idx] = seg
            deps = {}
            for k in op.reads:
                p = last_w.get(k)
                if p is not None:
                    deps[p.idx] = p
            for k in op.writes:
                p = last_w.get(k)
                if p is not None:
                    deps[p.idx] = p
                for rd in readers.get(k, ()):
                    deps[rd.idx] = rd
            for k in op.reads:
                readers.setdefault(k, []).append(op)
            for k in op.writes:
                last_w[k] = op
                readers[k] = []
            deps.pop(op.idx, None)
            deps_of[op.idx] = deps
        nseg = seg + 1
        streams = {e: [] for e in self.ENGS}
        by_seg = [[] for _ in range(nseg)]
        for op in ops:
            by_seg[seg_of[op.idx]].append(op)
        DMA_LAT = 2500.0
        for sops in by_seg:
            if not sops:
                continue
            if len(sops) == 1 and sops[0].is_bar or not reorder:
                for op in sops:
                    streams[op.eng].append(op)
                continue
            indeg = {}
            succ = {}
            for op in sops:
                indeg[op.idx] = len(deps_of[op.idx])
                for p in deps_of[op.idx].values():
                    succ.setdefault(p.idx, []).append(op)
            fin = {}
            free = {e: 0.0 for e in self.ENGS}
            heaps = {e: [] for e in self.ENGS}
            pe_grp0 = {}
            for op in sops:
                if indeg[op.idx] == 0:
                    heapq.heappush(heaps[op.eng], (0.0, op.idx, op))
                    if op.eng == "pe":
                        heapq.heappush(pe_grp0.setdefault(op.grp, []), (op.idx, 0.0, op))
            left = len(sops)
            pe_grp = pe_grp0
            cur_grp = None
            while left:
                best = None
                for e in self.ENGS:
                    h_ = heaps[e]
                    while h_ and h_[0][2].done:
                        heapq.heappop(h_)
                    if h_:
                        rdy, idx, op = h_[0]
                        if e == "pe" and cur_grp is not None:
                            g_ = pe_grp.get(cur_grp)
                            while g_ and g_[0][2].done:
                                heapq.heappop(g_)
                            if g_ and g_[0][1] <= max(free[e], rdy) + 300.0:
                                idx, rdy, op = g_[0]
                        start = max(rdy, free[e])
                        if best is None or (start, idx) < (best[0], best[1]):
                            best = (start, idx, e, op, rdy)
                start, idx, e, op, rdy = best
                op.done = True
                if e == "pe":
                    cur_grp = op.grp
                cost = op.cost if op.cost is not None else self.DEF_COST[e]
                if op.kind == "d":
                    free[e] = start + 60.0
                    f = start + DMA_LAT + cost
                else:
                    free[e] = start + cost
                    f = free[e] + 60.0
                fin[idx] = f
                streams[e].append(op)
                left -= 1
                for q in succ.get(idx, ()):
                    indeg[q.idx] -= 1
                    if indeg[q.idx] == 0:
                        r = max(fin[p] for p in deps_of[q.idx])
                        heapq.heappush(heaps[q.eng], (r, q.idx, q))
                        if q.eng == "pe":
                            heapq.heappush(pe_grp.setdefault(q.grp, []), (q.idx, r, q))
        for e in self.ENGS:
            for i, op in enumerate(streams[e]):
                op.lidx = i
        cur_bar = None
        last_c = {}
        pend_d = []
        order = sorted(ops, key=lambda o: (seg_of[o.idx], o.idx))
        for op in order:
            if op.is_bar:
                for e in self.ENGS:
                    cands = [o for o in streams[e] if seg_of[o.idx] == seg_of[op.idx] - 1
                             and o.kind == "c"]
                    if cands:
                        p = cands[-1]
                        p.signal = True
                        op.waits.append(p)
                for p in pend_d:
                    op.waits.append(p)
                pend_d = []
                op.signal = True
                cur_bar = op
                continue
            if cur_bar is not None:
                op.waits.append(cur_bar)
            if op.kind == "d":
                pend_d.append(op)
            best_c = {}
            for p in deps_of[op.idx].values():
                if p.kind == "d":
                    op.waits.append(p)
                    continue
                if p.eng == op.eng and op.kind == "c":
                    if op.eng == "pe":
                        continue
                    if op.eng != "pool" and op.lidx - p.lidx > 3:
                        continue
                q = best_c.get(p.eng)
                if q is None or p.lidx > q.lidx:
                    best_c[p.eng] = p
            for p in best_c.values():
                p.signal = True
                op.waits.append(p)
        nsig = {e: 0 for e in self.ENGS}
        ndma = {e: 0 for e in self.ENGS}
        for e in self.ENGS:
            for op in streams[e]:
                if op.kind == "d":
                    op.dma_i = ndma[e]
                    ndma[e] += 1
                elif op.signal:
                    op.cnt = nsig[e]
                    nsig[e] += 1
        self.csems = {}
        self.dsems = {}
        for e in self.ENGS:
            n = (nsig[e] + SEM_LIM - 1) // SEM_LIM
            self.csems[e] = [nc.alloc_semaphore(name=f"c_{e}_{i}") for i in range(n)]
            if ndma[e]:
                self.dsems[e] = [nc.alloc_semaphore(name=f"d_{e}_{i}")
                                 for i in range(min(N_DMA_SEMS, ndma[e]))]
        self.streams = streams
        self.stats = dict(nsig=nsig, ndma=ndma,
                          nops={e: len(streams[e]) for e in self.ENGS})

    def _sem_val(self, p):
        if p.kind == "d":
            ns = len(self.dsems[p.eng])
            return self.dsems[p.eng][p.dma_i % ns], 16 * (p.dma_i // ns + 1)
        return self.csems[p.eng][p.cnt // SEM_LIM], (p.cnt % SEM_LIM) + 1

    def emit_engine(self, e, eng, tail_waits=()):
        waited = {}

        def do_wait(sem, val):
            key = sem.num
            if waited.get(key, 0) >= val:
                return
            waited[key] = val
            eng.wait_ge(sem, val)

        for op in self.streams[e]:
            for p in op.waits:
                sem, val = self._sem_val(p)
                do_wait(sem, val)
            if op.kind == "d":
                ns = len(self.dsems[e])
                if op.dma_i >= ns:
                    do_wait(self.dsems[e][op.dma_i % ns], 16 * (op.dma_i // ns))
                ins = op.fn(eng)
                sem, val = self._sem_val(op)
                ins.then_inc(sem, 16)
            else:
                ins = op.fn(eng)
                if op.signal:
                    sem, val = self._sem_val(op)
                    ins.then_inc(sem, 1)
        for p in tail_waits:
            sem, val = self._sem_val(p)
            do_wait(sem, val)

    def emit(self, tail_ops=()):
        nc = self.nc
        with nc.Block() as block:
            @block.tensor
            def _(eng):
                self.emit_engine("pe", eng)

            @block.scalar
            def _(eng):
                self.emit_engine("act", eng)

            @block.vector
            def _(eng):
                self.emit_engine("dve", eng)

            @block.gpsimd
            def _(eng):
                self.emit_engine("pool", eng)

            @block.sync
            def _(eng):
                self.emit_engine("sp", eng, tail_waits=tail_ops)


class Alloc:
    def __init__(self, nc, base=16512, limit=229300):
        self.nc = nc
        self.off = base
        self.limit = limit
        self.n = 0

    def t(self, shape, dtype, name=None):
        nbytes = int(np.prod(shape[1:])) * (2 if dtype == BF16 else 4)
        off = (self.off + 63) // 64 * 64
        assert off + nbytes <= self.limit, (name, off, nbytes, self.limit)
        self.off = off + nbytes
        self.n += 1
        h = self.nc.alloc_sbuf_tensor_at(f"{name or 't'}_{id(self) % 9973}_{self.n}", list(shape),
                                         dtype, offset=off)
        return h.ap()

    def mark(self):
        return self.off

    def reset(self, m):
        self.off = m


def bcast(ap, shape):
    return ap.broadcast_to(list(shape))


def build(debug=False, stop_after="M"):
    nc = bass.Bass("TRN2", target_bir_lowering=False)
    S = Sched(nc)

    def din(name, shape, dt=F32):
        return nc.dram_tensor(name, list(shape), dt, kind="ExternalInput").ap()

    def dscr(name, shape, dt=F32):
        kind = "ExternalOutput" if debug else "Internal"
        return nc.dram_tensor(name, list(shape), dt, kind=kind).ap()

    x_d = din("x", [SEQ, D])
    ctx_d = din("ctx", [CTX, D])
    cvec_d = din("cvec", [128, 16])
    wmod_d = din("w_mod", [D, 3 * D])
    bmodfm_d = din("bmod_fm", [128, 24])
    rows_d = din("rows", [128, 3 * D])
    win_d = din("w_in", [D, IN_COLS])
    binfm_d = din("bin_fm", [128, 80])
    small_d = din("small_rows", [128, 64])
    cqkv_d = din("cqkv_fm", [128, 72])
    cb_d = din("cb_fm", [128, 32])
    onw_d = din("onw", [128, 1])
    wa_d = din("w_a", [D, D])
    wb_d = din("w_b", [D, D])
    wo_d = din("w_out", [D, D])
    consts_d = din("consts", [128, 1024])
    out_d = nc.dram_tensor("out", [SEQ, D], F32, kind="ExternalOutput").ap()

    qT_d = dscr("qT_s", [128, NCH, H, CH], BF16)
    kT_d = dscr("kT_s", [128, NCH, H, CH], BF16)
    ktok_d = dscr("ktok_s", [NCH, CH, H, DH])
    vtok_d = dscr("vtok_s", [NCH, CH, H, DH])
    gb_d = dscr("gb_s", [NTOK, 32])
    za_d = dscr("za_s", [128, H, SEQ])
    yb_d = dscr("yb_s", [128, 8, SEQ], BF16)
    sg_d = dscr("sg_s", [128, 16, SEQ])
    o_d = dscr("o_s", [2, 128, H, SEQ])
    hx_dbg = dscr("hx_dbg", [128, 8, NTOK], BF16) if debug else None

    PT = [nc.alloc_psum_tensor(f"pt{i}", [128, 1024], F32).ap() for i in range(4)]

    def pbank(i):
        return PT[i // 2][:, (i % 2) * 512:(i % 2) * 512 + 512], f"ps{i}"

    A = Alloc(nc)
    consts = A.t([128, 1024], F32, "consts")
    ident = consts[:, 0:128]
    ones = consts[:, 128:256]
    tri = [consts[0:64, 256:320], consts[0:64, 320:384]]
    def mask_ap(d, which):
        o = 384 + (d * 2 + which) * 64
        return consts[0:64, o:o + 64]
    LN_EPS_AP = consts[:, 640:641]
    L2_EPS_AP = consts[:, 641:642]
    RMS_EPS_AP = consts[:, 642:643]
    ONE_AP = consts[:, 643:644]
    ZERO_AP = consts[:, 644:645]
    binfm = A.t([128, 80], F32, "binfm")
    small = A.t([128, 64], F32, "small")
    cqkv = A.t([128, 72], F32, "cqkv")
    cb = A.t([128, 32], F32, "cb")
    onw = A.t([128, 1], F32, "onw")
    cvec = A.t([128, 16], F32, "cvec")
    scv = A.t([128, 16], F32, "scv")
    bmodfm = A.t([128, 24], F32, "bmodfm")
    mod = A.t([128, 48], F32, "mod")
    sc1 = A.t([128, 16], F32, "sc1")
    negar = A.t([128, 16], F32, "negar")
    rows = A.t([128, 3 * D], F32, "rows")
    bar_t = A.t([128, 8], F32, "bar")
    persist_mark = A.mark()

    def phase_barrier():
        S.barrier(lambda e: e.memset(bar_t, 0.0))

    ld = []
    for (dst, src, key) in [(consts, consts_d, "consts"), (binfm, binfm_d, "binfm"),
                            (small, small_d, "small"), (cqkv, cqkv_d, "cqkv"), (cb, cb_d, "cb"),
                            (onw, onw_d, "onw"), (cvec, cvec_d, "cvec"),
                            (bmodfm, bmodfm_d, "bmodfm"), (rows, rows_d, "rows")]:
        S.dma("sp", (lambda e, dst=dst, src=src: e.dma_start(out=dst, in_=src)), w=[key])

    m0 = A.mark()
    wms = [A.t([128, 8, 1024], F32, f"wm{i}") for i in range(2)]
    screp = A.t([128, 8, 128], F32, "screp")
    wmod_r = wmod_d.rearrange("(kc p) n -> p kc n", p=128)
    S.add("act", lambda e: e.activation(out=scv, in_=cvec, func=AF.Silu), r=["cvec"], w=["scv"])
    scv3 = scv.rearrange("p (k j) -> p k j", j=2)
    S.add("pool", lambda e: e.tensor_copy(out=screp, in_=bcast(scv3[:, :, 0:1], [128, 8, 128])),
          r=["scv"], w=["screp"])
    S.add("act", lambda e: e.activation(out=negar, in_=small[:, 32:48], func=AF.Exp),
          r=["small"], w=["negar"])
    S.add("pool", lambda e: e.tensor_scalar(out=negar, in0=negar, scalar1=-1.0, scalar2=None,
                                            op0=ALU.mult), r=["negar"], w=["negar"])
    pm, pmk = pbank(0)
    pg0, pg0k = pbank(2)
    pg1, pg1k = pbank(3)
    for piece in range(3):
        wm, wmk = wms[piece % 2], f"wm{piece % 2}"
        S.dma("sp", (lambda e, piece=piece, wm=wm: e.dma_start(
            out=wm, in_=wmod_r[:, :, piece * 1024:(piece + 1) * 1024])), w=[wmk])
        for jj in range(8):
            j = piece * 8 + jj
            for kc in range(8):
                S.add("pe", (lambda e, j=j, jj=jj, kc=kc, wm=wm: e.matmul(
                    pm[:, j * 2:j * 2 + 2], lhsT=wm[:, kc, jj * 128:(jj + 1) * 128],
                    rhs=scv3[:, kc, :], start=(kc == 0), stop=(kc == 7))),
                    r=[wmk, "scv"], w=[pmk])
        if piece == 2:
            for half, (pg, pgk) in enumerate([(pg0, pg0k), (pg1, pg1k)]):
                for kc in range(8):
                    S.add("pe", (lambda e, half=half, kc=kc, pg=pg, wm=wm: e.matmul(
                        pg, lhsT=screp[:, kc, :], rhs=wm[:, kc, half * 512:(half + 1) * 512],
                        start=(kc == 0), stop=(kc == 7))), r=[wmk, "screp"], w=[pgk])
                S.add("dve", (lambda e, half=half, pg=pg: e.tensor_tensor(
                    out=rows[:, half * 512:(half + 1) * 512], in0=pg,
                    in1=rows[:, half * 512:(half + 1) * 512], op=ALU.add)),
                    r=[pgk, "rows"], w=["rows", pgk])
    mod3 = mod.rearrange("p (j t) -> p j t", t=2)
    S.add("dve", lambda e: e.tensor_tensor(
        out=mod3, in0=pm[:, 0:48].rearrange("p (j t) -> p j t", t=2),
        in1=bcast(bmodfm.unsqueeze(2), [128, 24, 2]), op=ALU.add),
        r=[pmk, "bmodfm"], w=["mod", pmk])
    S.add("dve", lambda e: e.tensor_scalar(out=sc1, in0=mod[:, 16:32], scalar1=1.0, scalar2=None,
                                           op0=ALU.add), r=["mod"], w=["sc1"])
    sc13 = sc1.rearrange("p (k j) -> p k j", j=2)
    A.reset(m0)
    phase_barrier()

    hxT = A.t([128, 8, NTOK], BF16, "hxT")
    mL = A.mark()
    NB = 2
    xt = [A.t([128, D], F32, f"xt{i}") for i in range(NB)]
    xn = [A.t([128, D], F32, f"xn{i}") for i in range(NB)]
    tmpL = [A.t([128, 512], F32, f"tmpL{i}") for i in range(2)]
    stats = [A.t([128, 16], F32, f"st{i}") for i in range(NB)]
    for i in range(NTOK // 128):
        b = i % NB
        jx = 1 if i < 2 else 0
        src = ctx_d[i * 128:(i + 1) * 128, :] if i < 2 else x_d[(i - 2) * 128:(i - 1) * 128, :]
        kx, kn, kst = f"xt{b}", f"xn{b}", f"st{b}"
        S.dma("sp", (lambda e, b=b, src=src: e.dma_start(out=xt[b], in_=src)), w=[kx])
        st = stats[b]
        S.add("dve", (lambda e, b=b, st=st: e.bn_stats(out=st[:, 0:6], in_=xt[b][:, 0:512])),
              r=[kx], w=[kst + "a"])
        S.add("dve", (lambda e, b=b, st=st: e.bn_stats(out=st[:, 6:12], in_=xt[b][:, 512:1024])),
              r=[kx], w=[kst + "b"])
        S.add("dve", (lambda e, st=st: e.bn_aggr(out=st[:, 12:14], in_=st[:, 0:12])),
              r=[kst + "a", kst + "b"], w=[kst + "mv"])
        S.add("act", (lambda e, st=st: e.activation(out=st[:, 14:15], in_=st[:, 13:14], func=AF.Ln,
                                                    bias=LN_EPS_AP, scale=1.0)),
              r=[kst + "mv", "consts"], w=[kst + "r0"])
        S.add("act", (lambda e, st=st: e.activation(out=st[:, 14:15], in_=st[:, 14:15], func=AF.Exp,
                                                    scale=-0.5)), r=[kst + "r0"], w=[kst + "r"])
        S.add("dve", (lambda e, st=st: e.scalar_tensor_tensor(
            out=st[:, 15:16], in0=st[:, 12:13], scalar=-1.0, in1=st[:, 14:15],
            op0=ALU.mult, op1=ALU.mult)), r=[kst + "mv", kst + "r"], w=[kst + "n"])
        S.add("act", (lambda e, b=b, st=st: e.activation(
            out=xn[b], in_=xt[b], func=AF.Identity, bias=st[:, 15:16], scale=st[:, 14:15])),
            r=[kx, kst + "r", kst + "n"], w=[kn])
        for half in range(2):
            pb, pbk = pbank(half)
            for q4 in range(4):
                kc = half * 4 + q4
                S.add("pe", (lambda e, b=b, kc=kc, q4=q4, pb=pb: e.transpose(
                    out=pb[:, q4 * 128:(q4 + 1) * 128], in_=xn[b][:, kc * 128:(kc + 1) * 128],
                    identity=ident)), r=[kn, "consts"], w=[pbk])
            tl, tlk = tmpL[half], f"tmpL{half}"
            S.add("dve", (lambda e, half=half, pb=pb, tl=tl, jx=jx: e.tensor_tensor(
                out=tl.rearrange("p (k t) -> p k t", t=128),
                in0=pb.rearrange("p (k t) -> p k t", t=128),
                in1=bcast(sc13[:, half * 4:half * 4 + 4, jx:jx + 1], [128, 4, 128]), op=ALU.mult)),
                r=[pbk, "sc1"], w=[tlk, pbk])
            S.add("pool", (lambda e, half=half, tl=tl, jx=jx, i=i: e.tensor_tensor(
                out=hxT[:, half * 4:half * 4 + 4, i * 128:(i + 1) * 128],
                in0=tl.rearrange("p (k t) -> p k t", t=128),
                in1=bcast(mod3[:, half * 4:half * 4 + 4, jx:jx + 1], [128, 4, 128]), op=ALU.add)),
                r=[tlk, "mod"], w=["hxT"])
    if debug:
        S.dma("sp", lambda e: e.dma_start(out=hx_dbg, in_=hxT), r=["hxT"], w=["hx_dbg"])
    A.reset(mL)
    phase_barrier()
    tail = []
    if stop_after == "L":
        return nc, S, tail

    mP = A.mark()
    NW = 8
    wbuf = [A.t([128, 8, 128], BF16, f"wb{i}") for i in range(NW)]
    wab = A.t([128, 8, 32], BF16, "wab")
    win_r = win_d.rearrange("(kc p) n -> p kc n", p=128)
    wslot = [0]

    def load_w(col0, width=128, dst=None):
        if dst is None:
            s = wslot[0] % NW
            wslot[0] += 1
            dst, key = wbuf[s], f"wb{s}"
        else:
            key = "wab"
        S.dma("pool", (lambda e, dst=dst, col0=col0, width=width: e.dma_start(
            out=dst, in_=win_r[:, :, col0:col0 + width])), w=[key])
        return dst, key

    NT = 4
    wt = [[A.t([128, 512], F32, f"w{s}_{i}") for i in range(6)] for s in range(NT)]
    wtb = [A.t([128, 512], BF16, f"wtb{s}") for s in range(NT)]
    wtt = [A.t([128, 512], F32, f"wtt{s}") for s in range(NT)]
    tset = [0]

    def blocks(with_ctx):
        res = []
        if with_ctx:
            res.append((0, 0, CTX, CTX))
        for bl in range(8):
            res.append((bl + 1, CTX + bl * 512, 512, 64))
        return res

    def proj(wk, w_ap, tok0, N, bank):
        pb, pbk = pbank(bank)
        for kc in range(8):
            S.add("pe", (lambda e, kc=kc, pb=pb, w_ap=w_ap, tok0=tok0, N=N: e.matmul(
                pb[:, 0:N], lhsT=w_ap[:, kc, :], rhs=hxT[:, kc, tok0:tok0 + N],
                start=(kc == 0), stop=(kc == 7))), r=[wk, "hxT"], w=[pbk], c=400.0)
        return pb, pbk

    def conv3(src, srck, dst, dstk, N, rowlen, w0, w1, w2, extra=None, cwk="cqkv"):
        if extra is None:
            S.add("dve", (lambda e: e.tensor_scalar(out=dst[:, 0:N], in0=src[:, 0:N], scalar1=w1,
                                                    scalar2=None, op0=ALU.mult)),
                  r=[srck, cwk], w=[dstk])
        else:
            S.add("dve", (lambda e: e.tensor_scalar(out=dst[:, 0:N], in0=src[:, 0:N], scalar1=w1,
                                                    scalar2=extra, op0=ALU.mult, op1=ALU.add)),
                  r=[srck, cwk], w=[dstk])
        s3 = src[:, 0:N].rearrange("p (r t) -> p r t", t=rowlen)
        d3 = dst[:, 0:N].rearrange("p (r t) -> p r t", t=rowlen)
        S.add("dve", (lambda e: e.scalar_tensor_tensor(
            out=d3[:, :, 1:rowlen], in0=s3[:, :, 0:rowlen - 1], scalar=w0, in1=d3[:, :, 1:rowlen],
            op0=ALU.mult, op1=ALU.add)), r=[srck, dstk, cwk], w=[dstk])
        S.add("dve", (lambda e: e.scalar_tensor_tensor(
            out=d3[:, :, 0:rowlen - 1], in0=s3[:, :, 1:rowlen], scalar=w2,
            in1=d3[:, :, 0:rowlen - 1], op0=ALU.mult, op1=ALU.add)), r=[srck, dstk, cwk],
            w=[dstk])

    nbinfm = A.t([128, 80], F32, "nbinfm")
    S.add("pool", lambda e: e.tensor_scalar(out=nbinfm, in0=binfm, scalar1=-1.0, scalar2=None,
                                            op0=ALU.mult), r=["binfm"], w=["nbinfm"])

    def sigmoid_chain(dst, dstk, src, srck, nbias, N):
        S.add("act", (lambda e: e.activation(out=dst[:, 0:N], in_=src[:, 0:N], func=AF.Exp,
                                             bias=nbias, scale=-1.0)),
              r=[srck, "nbinfm"], w=[dstk] + ([srck] if srck.startswith("ps") else []))
        S.add("act", (lambda e: e.activation(out=dst[:, 0:N], in_=dst[:, 0:N], func=AF.Ln,
                                             bias=ONE_AP, scale=1.0)), r=[dstk, "consts"], w=[dstk])
        S.add("act", (lambda e: e.activation(out=dst[:, 0:N], in_=dst[:, 0:N], func=AF.Exp,
                                             scale=-1.0)), r=[dstk], w=[dstk])

    class Grp:
        def __init__(self, cols):
            self.cols = cols
            self.w = None

        def ensure(self):
            if self.w is None:
                self.w = [load_w(c) for c in cols_of(self)]
            return self.w

    def cols_of(g):
        return g.cols

    groups = []

    def new_set():
        s = tset[0] % NT
        tset[0] += 1
        return s

    def qkv_block(gi, kind, h, blk):
        g = kind * 8 + h
        bl, tok0, N, rowlen = blk
        (w_ap, wk), = groups[gi].ensure()
        if gi + 1 < len(groups):
            groups[gi + 1].ensure()
        bias = binfm[:, g:g + 1]
        zero_b = ZERO_AP
        cw0, cw1, cw2 = (cqkv[:, g * 3 + i:g * 3 + i + 1] for i in range(3))
        s = new_set()
        T = wt[s]
        K = [f"w{s}_{i}" for i in range(6)]
        n0 = tok0 // CH
        nch = N // CH
        pb, pbk = proj(wk, w_ap, tok0, N, (0, 1, 6, 7)[s % 4])
        yield
        S.add("act", (lambda e: e.activation(out=T[0][:, 0:N], in_=pb[:, 0:N], func=AF.Identity,
                                             bias=bias, scale=1.0)),
              r=[pbk, "binfm"], w=[K[0], pbk])
        yield
        conv3(T[0], K[0], T[1], K[1], N, rowlen, cw0, cw1, cw2)
        yield
        sigmoid_chain(T[2], K[2], T[1], K[1], zero_b, N)
        yield
        S.add("dve", (lambda e: e.tensor_tensor(out=T[2][:, 0:N], in0=T[2][:, 0:N],
                                                in1=T[1][:, 0:N], op=ALU.mult)),
              r=[K[1], K[2]], w=[K[2]])
        fin, fink = T[2], K[2]
        if kind < 2:
            S.add("pool", (lambda e: e.tensor_tensor(out=T[3][:, 0:N], in0=T[2][:, 0:N],
                                                     in1=T[2][:, 0:N], op=ALU.mult)),
                  r=[K[2]], w=[K[3]])
            yield
            p2, p2k = pbank((2, 3, 4, 5)[s % 4])
            S.add("pe", (lambda e: e.matmul(p2[:, 0:N], lhsT=ones, rhs=T[3][:, 0:N], start=True,
                                            stop=True)), r=[K[3], "consts"], w=[p2k], c=1200.0)
            yield
            S.add("act", (lambda e: e.activation(out=T[4][:, 0:N], in_=p2[:, 0:N], func=AF.Ln,
                                                 bias=L2_EPS_AP, scale=1.0)),
                  r=[p2k, "consts"], w=[K[4], p2k])
            S.add("act", (lambda e: e.activation(out=T[4][:, 0:N], in_=T[4][:, 0:N], func=AF.Exp,
                                                 scale=-0.5)), r=[K[4]], w=[K[4]])
            yield
            qs = float(DH ** -0.5) if kind == 0 else 1.0
            if kind == 0:
                S.add("dve", (lambda e: e.scalar_tensor_tensor(
                    out=wtb[s][:, 0:N], in0=T[2][:, 0:N], scalar=qs, in1=T[4][:, 0:N],
                    op0=ALU.mult, op1=ALU.mult)), r=[K[2], K[4]], w=[f"wtb{s}"])
            else:
                S.add("dve", (lambda e: e.scalar_tensor_tensor(
                    out=T[5][:, 0:N], in0=T[2][:, 0:N], scalar=qs, in1=T[4][:, 0:N],
                    op0=ALU.mult, op1=ALU.mult)), r=[K[2], K[4]], w=[K[5]])
                fin, fink = T[5], K[5]
                S.add("pool", (lambda e: e.tensor_copy(out=wtb[s][:, 0:N], in_=T[5][:, 0:N])),
                      r=[K[5]], w=[f"wtb{s}"])
            dstT = (qT_d if kind == 0 else kT_d)
            S.dma("sp", (lambda e: e.dma_start(
                out=dstT[:, n0:n0 + nch, h, :],
                in_=wtb[s][:, 0:N].rearrange("p (n t) -> p n t", t=CH))),
                r=[f"wtb{s}"],
                w=[f"{'qT' if kind == 0 else 'kT'}_d{n}_{h}" for n in range(n0, n0 + nch)])
        if kind >= 1:
            yield
            nsub = N // 128
            p3, p3k = pbank((4, 5, 2, 3)[s % 4])
            for sub in range(nsub):
                S.add("pe", (lambda e, sub=sub: e.transpose(
                    out=p3[:, sub * 128:(sub + 1) * 128], in_=fin[:, sub * 128:(sub + 1) * 128],
                    identity=ident)), r=[fink, "consts"], w=[p3k])
            yield
            S.add("act", (lambda e: e.activation(out=wtt[s][:, 0:N], in_=p3[:, 0:N],
                                                 func=AF.Identity)), r=[p3k], w=[f"wtt{s}", p3k])
            dst = ktok_d if kind == 1 else vtok_d
            S.dma("sp", (lambda e: e.dma_start(
                out=dst[n0:n0 + nch, :, h, :].rearrange("(s n2) t d -> (n2 t) s d", s=nsub),
                in_=wtt[s][:, 0:N].rearrange("p (s d) -> p s d", d=128))),
                r=[f"wtt{s}"],
                w=[f"{'ktok' if kind == 1 else 'vtok'}_d{n}_{h}" for n in range(n0, n0 + nch)])

    def act_block(gi, bgrp, is_silu, dst_fn, dkey, blk):
        bl, tok0, N, rowlen = blk
        (w_ap, wk), = groups[gi].ensure()
        if gi + 1 < len(groups):
            groups[gi + 1].ensure()
        s = new_set()
        T = wt[s]
        pb, pbk = proj(wk, w_ap, tok0, N, (0, 1, 6, 7)[s % 4])
        yield
        sigmoid_chain(T[0], f"w{s}_0", pb, pbk, nbinfm[:, bgrp:bgrp + 1], N)
        if is_silu:
            S.add("dve", (lambda e: e.scalar_tensor_tensor(
                out=T[0], in0=pb, scalar=binfm[:, bgrp:bgrp + 1], in1=T[0], op0=ALU.add,
                op1=ALU.mult)), r=[pbk, f"w{s}_0", "binfm"], w=[f"w{s}_0", pbk])
        yield
        l0 = tok0 - CTX
        S.dma("sp", (lambda e: e.dma_start(out=dst_fn(l0), in_=T[0])),
              r=[f"w{s}_0"], w=[f"{dkey}_{bl}"])

    def bb_block(gi, j, blk):
        bl, tok0, N, rowlen = blk
        (wx, wxk), (wbg, wbgk), (wcg, wcgk), (wzb, wzbk) = groups[gi].ensure()
        if gi + 1 < len(groups):
            groups[gi + 1].ensure()
        bx, bbg, bcg, bzb = (binfm[:, 32 + 8 * t + j:32 + 8 * t + j + 1] for t in range(4))
        nbzb = nbinfm[:, 56 + j:56 + j + 1]
        c0, c1, c2, cbb = (cb[:, j * 4 + i:j * 4 + i + 1] for i in range(4))
        s = new_set()
        T = wt[s]
        K = [f"w{s}_{i}" for i in range(6)]
        pb0 = 4 * (s % 2)
        px, pxk = proj(wxk, wx, tok0, N, pb0 + 0)
        pc, pck = proj(wcgk, wcg, tok0, N, pb0 + 1)
        yield
        S.add("act", (lambda e: e.activation(out=T[0], in_=px, func=AF.Identity, bias=bx,
                                             scale=1.0)), r=[pxk, "binfm"], w=[K[0], pxk])
        pbg, pbgk = proj(wbgk, wbg, tok0, N, pb0 + 2)
        pz, pzk = proj(wzbk, wzb, tok0, N, pb0 + 3)
        yield
        S.add("dve", (lambda e: e.scalar_tensor_tensor(
            out=T[1], in0=pc, scalar=bcg, in1=T[0], op0=ALU.add, op1=ALU.mult)),
            r=[pck, K[0], "binfm"], w=[K[1], pck])
        sigmoid_chain(T[4], K[4], pz, pzk, nbzb, N)
        yield
        conv3(T[1], K[1], T[2], K[2], N, 64, c0, c1, c2, extra=cbb, cwk="cb")
        S.add("dve", (lambda e: e.scalar_tensor_tensor(
            out=T[4], in0=pz, scalar=bzb, in1=T[4], op0=ALU.add, op1=ALU.mult)),
            r=[pzk, K[4], "binfm"], w=[K[4], pzk])
        yield
        S.add("dve", (lambda e: e.scalar_tensor_tensor(
            out=T[3], in0=pbg, scalar=bbg, in1=T[2], op0=ALU.add, op1=ALU.mult)),
            r=[pbgk, K[2], "binfm"], w=[K[3], pbgk])
        yield
        S.add("pool", (lambda e: e.tensor_tensor(out=wtb[s], in0=T[3], in1=T[4], op=ALU.mult)),
              r=[K[3], K[4]], w=[f"wtb{s}"])
        l0 = tok0 - CTX
        S.dma("sp", (lambda e: e.dma_start(out=yb_d[:, j, l0:l0 + 512], in_=wtb[s])),
              r=[f"wtb{s}"], w=[f"yb_d{j}_{bl}"])

    ptasks = []
    for kind in range(3):
        for h in range(H):
            gi = len(groups)
            groups.append(Grp([(kind * 8 + h) * 128]))
            for blk in blocks(True):
                ptasks.append((lambda gi=gi, kind=kind, h=h, blk=blk: qkv_block(gi, kind, h, blk)))
    for h in range(H):
        gi = len(groups)
        groups.append(Grp([3072 + h * 128]))
        for blk in blocks(False):
            ptasks.append((lambda gi=gi, h=h, blk=blk: act_block(
                gi, 24 + h, True, (lambda l0, h=h: za_d[:, h, l0:l0 + 512]), f"za_d{h}", blk)))
    for m in range(16):
        gi = len(groups)
        groups.append(Grp([8224 + m * 128]))
        for blk in blocks(False):
            ptasks.append((lambda gi=gi, m=m, blk=blk: act_block(
                gi, 64 + m, False, (lambda l0, m=m: sg_d[:, m, l0:l0 + 512]), f"sg_d{m}", blk)))
    for j in range(8):
        gi = len(groups)
        groups.append(Grp([4128 + j * 128, 5152 + j * 128, 6176 + j * 128, 7200 + j * 128]))
        for blk in blocks(False):
            ptasks.append((lambda gi=gi, j=j, blk=blk: bb_block(gi, j, blk)))

    def run_pipelined(tasks, depth):
        active = []
        it = iter(tasks)
        while True:
            while len(active) < depth:
                t = next(it, None)
                if t is None:
                    break
                active.append(t())
            if not active:
                break
            for g in list(active):
                try:
                    next(g)
                except StopIteration:
                    active.remove(g)

    nbb = 8 * 8
    run_pipelined(ptasks[:-nbb], 3)
    if tset[0] % 2:
        tset[0] += 1
    run_pipelined(ptasks[-nbb:], 2)

    load_w(4096, 32, dst=wab)
    abt = [A.t([128, 32], F32, f"abt{i}") for i in range(2)]
    abx = [A.t([128, 16], F32, f"abx{i}") for i in range(2)]
    gbt = [A.t([128, 32], F32, f"gbt{i}") for i in range(2)]
    for i in range(NTOK // 128):
        b = i % 2
        pb, pbk = pbank(6 + b)
        for kc in range(8):
            S.add("pe", (lambda e, kc=kc, pb=pb, i=i: e.matmul(
                pb[:, 0:32], lhsT=hxT[:, kc, i * 128:(i + 1) * 128], rhs=wab[:, kc, :],
                start=(kc == 0), stop=(kc == 7))), r=["wab", "hxT"], w=[pbk])
        S.add("dve", (lambda e, pb=pb, b=b: e.tensor_tensor(out=abt[b], in0=pb[:, 0:32],
                                                            in1=small[:, 0:32], op=ALU.add)),
              r=[pbk, "small"], w=[f"abt{b}", pbk])
        S.add("dve", (lambda e, b=b: e.tensor_tensor(out=abx[b], in0=abt[b][:, 0:16],
                                                     in1=small[:, 48:64], op=ALU.add)),
              r=[f"abt{b}", "small"], w=[f"abx{b}"])
        S.add("act", (lambda e, b=b: e.activation(out=abx[b], in_=abx[b], func=AF.Exp)),
              r=[f"abx{b}"], w=[f"abx{b}"])
        S.add("act", (lambda e, b=b: e.activation(out=abx[b], in_=abx[b], func=AF.Ln, bias=ONE_AP,
                                                  scale=1.0)), r=[f"abx{b}", "consts"], w=[f"abx{b}"])
        S.add("dve", (lambda e, b=b: e.tensor_tensor(out=gbt[b][:, 0:16], in0=abx[b], in1=negar,
                                                     op=ALU.mult)),
              r=[f"abx{b}", "negar"], w=[f"gbt{b}a"])
        S.add("act", (lambda e, b=b: e.activation(out=gbt[b][:, 16:32], in_=abt[b][:, 16:32],
                                                  func=AF.Exp, scale=-1.0)),
              r=[f"abt{b}"], w=[f"gbt{b}b"])
        S.add("act", (lambda e, b=b: e.activation(out=gbt[b][:, 16:32], in_=gbt[b][:, 16:32],
                                                  func=AF.Ln, bias=ONE_AP, scale=1.0)),
              r=[f"gbt{b}b", "consts"], w=[f"gbt{b}b"])
        S.add("act", (lambda e, b=b: e.activation(out=gbt[b][:, 16:32], in_=gbt[b][:, 16:32],
                                                  func=AF.Exp, scale=-1.0)),
              r=[f"gbt{b}b"], w=[f"gbt{b}b"])
        S.dma("sp", (lambda e, b=b, i=i: e.dma_start(out=gb_d[i * 128:(i + 1) * 128, :],
                                                     in_=gbt[b])),
              r=[f"gbt{b}a", f"gbt{b}b"], w=[f"gb_d{i}"])
    A.reset(mP)
    A.reset(persist_mark)
    phase_barrier()
    if stop_after == "P":
        return nc, S, tail

    S32 = A.t([128, 2, H, DH], F32, "S32")
    Sbf = A.t([128, 2, H, DH], BF16, "Sbf")
    S.add("pool", lambda e: e.memset(S32, 0.0), w=["S32_0", "S32_1"])
    S.add("pool", lambda e: e.memset(Sbf, 0.0), w=["Sbf_0", "Sbf_1"])
    identR = A.t([64, CH], F32R, "identR")
    S.add("dve", lambda e: e.tensor_copy(out=identR, in_=ident[0:64, 0:64]), r=["consts"],
          w=["identR"])
    identb = A.t([64, CH], BF16, "identb")
    S.add("dve", lambda e: e.tensor_copy(out=identb, in_=ident[0:64, 0:64]), r=["consts"],
          w=["identb"])

    class St:
        pass

    def mk_stream(d):
        st = St()
        st.d = d
        st.kT = [A.t([128, H, CH], BF16, f"kT{d}_{i}") for i in range(2)]
        st.qT = [A.t([128, H, CH], BF16, f"qT{d}_{i}") for i in range(2)]
        st.ktok = [A.t([64, H, DH], F32, f"ktok{d}_{i}") for i in range(2)]
        st.vtok = [A.t([64, H, DH], F32, f"vtok{d}_{i}") for i in range(2)]
        st.gb = [A.t([64, 32], F32, f"gb{d}_{i}") for i in range(2)]
        st.trig = A.t([64, H, CH], F32, f"trig{d}")
        st.gct = A.t([64, 16], F32, f"gct{d}")
        st.t = A.t([64, 2, H, CH], F32, f"t{d}")
        st.E = A.t([64, 2, H, CH], F32, f"E{d}")
        st.kd = A.t([64, 8], F32, f"kd{d}")
        st.dS = [A.t([128, 8], F32, f"dS{d}_{i}") for i in range(2)]
        st.eqb = A.t([128, H, CH], F32, f"eqb{d}")
        st.qd = [A.t([128, H, CH], BF16, f"qd{d}_{i}") for i in range(2)]
        st.X0f = [A.t([64, H + 1, CH], F32R, f"X0f{d}_{i}") for i in range(2)]
        st.Xb = [A.t([64, H + 1, CH], BF16, f"Xb{d}_{i}") for i in range(2)]
        st.Yb = [A.t([64, H + 1, CH], BF16, f"Yb{d}_{i}") for i in range(2)]
        st.Rt = [A.t([64, H + 1, CH], BF16, f"Rt{d}_{i}") for i in range(4)]
        st.Rf = A.t([64, H, CH], F32R, f"Rf{d}")
        st.IR = A.t([64, H, CH], F32, f"IR{d}")
        st.Eb = A.t([64, H, CH], BF16, f"Eb{d}")
        st.ETb = A.t([64, H + 1, CH], BF16, f"ETb{d}")
        st.Rb = A.t([64, H, CH], BF16, f"Rb{d}")
        st.vb = [A.t([64, H, DH], BF16, f"vb{d}_{i}") for i in range(2)]
        st.kbg = [A.t([64, H, DH], BF16, f"kbg{d}_{i}") for i in range(2)]
        st.kdec = [A.t([64, H, DH], BF16, f"kdec{d}_{i}") for i in range(2)]
        st.qkt = [A.t([64, H, CH], BF16, f"qkt{d}_{i}") for i in range(2)]
        st.u = [A.t([64, H, DH], F32, f"u{d}_{i}") for i in range(2)]
        st.wT = [A.t([128, H, CH], BF16, f"wT{d}_{i}") for i in range(2)]
        st.vnew = A.t([64, H, DH], BF16, f"vnew{d}")
        st.ost = [A.t([128, H, CH], F32, f"ost{d}_{i}") for i in range(2)]
        return st

    def g_prep(st, order):
        d = st.d
        B0, B1, B2, B3 = (pbank(d * 4 + i) for i in range(4))
        P1 = PT[d * 2 + 1]
        k_ = lambda nm: f"{nm}{d}"

        def L2(t, h):
            return t[:, h:h + 2, :].rearrange("p a c -> p (a c)")

        def F(t):
            return t[:, 0:H, :].rearrange("p h c -> p (h c)")
        last = CH - 1 if d == 0 else 0
        def issue_loads(step):
            n = order[step]
            b = step % 2
            is_lat = n >= 4
            kTk, qTk, ktokk, vtokk, gbk = (f"{nm}{d}_{b}" for nm in ("kT", "qT", "ktok", "vtok",
                                                                     "gb"))
            kT, qT, ktok, vtok, gb = st.kT[b], st.qT[b], st.ktok[b], st.vtok[b], st.gb[b]
            S.dma("sp", (lambda e, kT=kT, n=n: e.dma_start(out=kT, in_=kT_d[:, n, :, :])),
                  r=[f"kT_d{n}_{h}" for h in range(H)], w=[kTk])
            if is_lat:
                S.dma("sp", (lambda e, qT=qT, n=n: e.dma_start(out=qT, in_=qT_d[:, n, :, :])),
                      r=[f"qT_d{n}_{h}" for h in range(H)], w=[qTk])
            S.dma("sp", (lambda e, ktok=ktok, n=n: e.dma_start(out=ktok, in_=ktok_d[n])),
                  r=[f"ktok_d{n}_{h}" for h in range(H)], w=[ktokk])
            S.dma("sp", (lambda e, vtok=vtok, n=n: e.dma_start(out=vtok, in_=vtok_d[n])),
                  r=[f"vtok_d{n}_{h}" for h in range(H)], w=[vtokk])
            S.dma("sp", (lambda e, gb=gb, n=n: e.dma_start(out=gb, in_=gb_d[n * CH:(n + 1) * CH, :])),
                  r=[f"gb_d{n // 2}"], w=[gbk])

        issue_loads(0)

        def step_body(step, n):
            while rstep[d] < step - 1:
                yield "stall"
            b = step % 2
            is_lat = n >= 4
            kTk, qTk, ktokk, vtokk, gbk = (f"{nm}{d}_{b}" for nm in ("kT", "qT", "ktok", "vtok",
                                                                     "gb"))
            kT, qT, ktok, vtok, gb = st.kT[b], st.qT[b], st.ktok[b], st.vtok[b], st.gb[b]
            if step + 1 < len(order):
                issue_loads(step + 1)
            g_ap = gb[:, d * 8:d * 8 + 8]
            beta_ap = gb[:, 16 + d * 8:16 + d * 8 + 8]
            yield
            S.add("pool", (lambda e, g_ap=g_ap: e.tensor_tensor(
                out=st.trig, in0=bcast(tri[d].unsqueeze(1), [64, H, CH]),
                in1=bcast(g_ap.unsqueeze(2), [64, H, CH]), op=ALU.mult)),
                r=[gbk, "consts"], w=[k_("trig")])
            S.add("pe", (lambda e, g_ap=g_ap: e.matmul(B0[0][0:64, 0:8], lhsT=tri[d], rhs=g_ap,
                                                       start=True, stop=True)),
                  r=[gbk, "consts"], w=[B0[1]])
            S.add("act", (lambda e: e.activation(out=st.gct[:, 0:8], in_=B0[0][0:64, 0:8],
                                                 func=AF.Identity)),
                  r=[B0[1]], w=[k_("gct"), B0[1]])
            S.add("pe", (lambda e: e.matmul(B0[0], lhsT=ones[0:64, :],
                                            rhs=st.trig.rearrange("p h c -> p (h c)"),
                                            start=True, stop=True)),
                  r=[k_("trig"), "consts"], w=[B0[1]], c=950.0)
            gcb3 = B0[0][0:64, :].rearrange("p (h c) -> p h c", c=CH)
            gcbf = B0[0].rearrange("p (h c) -> p h c", c=CH)
            gct_b = bcast(st.gct[:, 0:8].unsqueeze(2), [64, H, CH])
            S.add("dve", (lambda e: e.scalar_tensor_tensor(
                out=st.t[:, 0], in0=gcb3, scalar=-1.0, in1=gct_b, op0=ALU.mult, op1=ALU.add)),
                r=[B0[1], k_("gct")], w=[k_("t0"), B0[1]])
            S.add("dve", (lambda e: e.scalar_tensor_tensor(
                out=st.t[:, 1], in0=gct_b, scalar=-1.0, in1=gcb3, op0=ALU.mult, op1=ALU.add)),
                r=[B0[1], k_("gct")], w=[k_("t1"), B0[1]])
            for wh in range(2):
                S.add("pool", (lambda e, wh=wh: e.tensor_tensor(
                    out=st.t[:, wh], in0=st.t[:, wh],
                    in1=bcast(mask_ap(d, wh).unsqueeze(1), [64, H, CH]), op=ALU.add)),
                    r=[k_(f"t{wh}"), "consts"], w=[k_(f"t{wh}")])
            S.add("act", (lambda e: e.activation(out=st.E.rearrange("p a h c -> p (a h c)"),
                                                 in_=st.t.rearrange("p a h c -> p (a h c)"),
                                                 func=AF.Exp)),
                  r=[k_("t0"), k_("t1")], w=[k_("E0"), k_("E1")])
            S.add("pool", (lambda e, beta_ap=beta_ap: e.tensor_tensor(
                out=st.E[:, 0], in0=st.E[:, 0], in1=bcast(beta_ap.unsqueeze(2), [64, H, CH]),
                op=ALU.mult)), r=[k_("E0"), gbk], w=[k_("E0")])
            S.add("dve", (lambda e: e.tensor_tensor(out=st.kd, in0=gcb3[:, :, last],
                                                    in1=st.gct[:, 0:8], op=ALU.subtract)),
                  r=[B0[1], k_("gct")], w=[k_("kd"), B0[1]])
            S.add("act", (lambda e: e.activation(out=st.kd, in_=st.kd, func=AF.Exp)),
                  r=[k_("kd")], w=[k_("kd")])
            S.add("act", (lambda e: e.activation(out=st.gct[:, 8:16], in_=st.gct[:, 0:8],
                                                 func=AF.Exp)), r=[k_("gct")], w=[k_("coef")])
            S.add("pool", (lambda e, beta_ap=beta_ap: e.tensor_tensor(
                out=st.gct[:, 8:16], in0=st.gct[:, 8:16], in1=beta_ap, op=ALU.mult)),
                r=[k_("coef"), gbk], w=[k_("coef")])
            S.add("act", (lambda e: e.activation(out=st.dS[b], in_=gcbf[:, :, last], func=AF.Exp)),
                  r=[B0[1]], w=[k_(f"dS{b}_"), B0[1]])
            if is_lat:
                S.add("act", (lambda e: e.activation(out=st.eqb.rearrange("p h c -> p (h c)"),
                                                     in_=B0[0], func=AF.Exp)),
                      r=[B0[1]], w=[k_("eqb"), B0[1]])
                S.add("pool", (lambda e, qT=qT: e.tensor_tensor(out=st.qd[b], in0=qT, in1=st.eqb,
                                                                op=ALU.mult)),
                      r=[qTk, k_("eqb")], w=[k_(f"qd{b}_")])
            S.add("pool", (lambda e, vtok=vtok, beta_ap=beta_ap: e.tensor_tensor(
                out=st.vb[b], in0=vtok, in1=bcast(beta_ap.unsqueeze(2), [64, H, DH]), op=ALU.mult)),
                r=[vtokk, gbk], w=[k_(f"vb{b}_")])
            S.add("pool", (lambda e, ktok=ktok: e.tensor_tensor(
                out=st.kbg[b], in0=ktok, in1=bcast(st.gct[:, 8:16].unsqueeze(2), [64, H, DH]),
                op=ALU.mult)), r=[ktokk, k_("coef")], w=[k_(f"kbg{b}_")])
            S.add("pool", (lambda e, ktok=ktok: e.tensor_tensor(
                out=st.kdec[b], in0=ktok, in1=bcast(st.kd.unsqueeze(2), [64, H, DH]), op=ALU.mult)),
                r=[ktokk, k_("kd")], w=[k_(f"kdec{b}_")])
            yield
            for h in range(H):
                S.add("pe", (lambda e, h=h, kT=kT: e.matmul(
                    B2[0][0:64, h * CH:(h + 1) * CH], lhsT=kT[:, h, :], rhs=kT[:, h, :],
                    start=True, stop=True)), r=[kTk], w=[B2[1]])
            if is_lat:
                for h in range(H):
                    S.add("pe", (lambda e, h=h, kT=kT, qT=qT: e.matmul(
                        B3[0][0:64, h * CH:(h + 1) * CH], lhsT=kT[:, h, :], rhs=qT[:, h, :],
                        start=True, stop=True)), r=[kTk, qTk], w=[B3[1]])
            yield
            X0f, X0k = st.X0f[b], k_(f"X0f{b}_")
            S.add("dve", (lambda e: e.tensor_tensor(
                out=X0f[:, 0:H, :].rearrange("p h c -> p (h c)"), in0=B2[0][0:64, :],
                in1=st.E[:, 0].rearrange("p h c -> p (h c)"), op=ALU.mult)),
                r=[B2[1], k_("E0")], w=[X0k, B2[1]])
            if is_lat:
                S.add("dve", (lambda e: e.tensor_tensor(
                    out=st.qkt[b].rearrange("p h c -> p (h c)"), in0=B3[0][0:64, :],
                    in1=st.E[:, 1].rearrange("p h c -> p (h c)"), op=ALU.mult)),
                    r=[B3[1], k_("E1")], w=[k_(f"qkt{b}_"), B3[1]])
            S.add("act", (lambda e: e.activation(out=F(st.Xb[0]),
                                                 in_=X0f[:, 0:H, :].rearrange("p h c -> p (h c)"),
                                                 func=AF.Identity)), r=[X0k], w=[k_("Xb0")])
            yield
            for h in range(H):
                S.add("pe", (lambda e, h=h: e.matmul(
                    B1[0][:, h * CH:(h + 1) * CH], lhsT=L2(st.Xb[0], h), rhs=identb,
                    start=True, stop=True)), r=[k_("Xb0"), "identb"], w=[B1[1]])
            yield
            S.add("act", (lambda e: e.activation(out=F(st.Yb[0]),
                                                 in_=B1[0][0:64, :], func=AF.Identity)),
                  r=[B1[1]], w=[k_("Yb0"), B1[1]])
            S.add("dve", (lambda e: e.scalar_tensor_tensor(
                out=st.Rt[2 * b + 0][:, 0:H, :], in0=B1[0][0:64, :].rearrange("p (h c) -> p h c", c=CH), scalar=-1.0,
                in1=bcast(ident[0:64, 0:64].unsqueeze(1), [64, H, CH]),
                op0=ALU.mult, op1=ALU.add)), r=[B1[1], "consts"], w=[k_(f"Rt{b}_0"), B1[1]])
            yield
            rp = 0
            for lvl in range(1, 6):
                pi, ci = (lvl - 1) % 2, lvl % 2
                Xp, Yp = st.Xb[pi], st.Yb[pi]
                Xn, Yn = st.Xb[ci], st.Yb[ci]
                Xpk, Ypk = k_(f"Xb{pi}"), k_(f"Yb{pi}")
                Xnk, Ynk = k_(f"Xb{ci}"), k_(f"Yb{ci}")
                for h in range(H):
                    S.add("pe", (lambda e, h=h, Xp=Xp, Yp=Yp: e.matmul(
                        B2[0][:, h * CH:(h + 1) * CH], lhsT=L2(Yp, h), rhs=Xp[:, h, :],
                        start=True, stop=True)), r=[Xpk, Ypk], w=[B2[1]])
                if lvl < 5:
                    for h in range(H):
                        S.add("pe", (lambda e, h=h, Xp=Xp, Yp=Yp: e.matmul(
                            B3[0][:, h * CH:(h + 1) * CH], lhsT=L2(Xp, h), rhs=Yp[:, h, :],
                            start=True, stop=True)), r=[Xpk, Ypk], w=[B3[1]])
                if lvl >= 2:
                    Rp, Rpk = st.Rt[2 * b + rp], k_(f"Rt{b}_{rp}")
                    for h in range(H):
                        S.add("pe", (lambda e, h=h, Xp=Xp, Rp=Rp: e.matmul(
                            B1[0][:, h * CH:(h + 1) * CH], lhsT=L2(Xp, h), rhs=Rp[:, h, :],
                            start=True, stop=True)), r=[Xpk, Rpk], w=[B1[1]])
                yield
                S.add("act", (lambda e, Xn=Xn: e.activation(out=F(Xn),
                                                            in_=B2[0][0:64, :], func=AF.Identity)),
                      r=[B2[1]], w=[Xnk, B2[1]])
                if lvl < 5:
                    S.add("dve", (lambda e, Yn=Yn: e.tensor_copy(
                        out=F(Yn), in_=B3[0][0:64, :])),
                        r=[B3[1]], w=[Ynk, B3[1]])
                if lvl >= 2:
                    Rn, Rnk = st.Rt[2 * b + 1 - rp], k_(f"Rt{b}_{1 - rp}")
                    S.add("dve", (lambda e, Rn=Rn, Rp=Rp: e.tensor_tensor(
                        out=F(Rn), in0=B1[0][0:64, :], in1=F(Rp), op=ALU.add)),
                        r=[B1[1], Rpk], w=[Rnk, B1[1]])
                    rp = 1 - rp
                yield
            assert rp == 0
            for h in range(H):
                S.add("pe", (lambda e, h=h: e.matmul(
                    B1[0][:, h * CH:(h + 1) * CH], lhsT=L2(st.Xb[1], h), rhs=st.Rt[2 * b + 0][:, h, :],
                    start=True, stop=True)), r=[k_("Xb1"), k_(f"Rt{b}_0")], w=[B1[1]])
            yield
            S.add("dve", (lambda e: e.tensor_tensor(
                out=F(st.Rt[2 * b + 1]), in0=B1[0][0:64, :], in1=F(st.Rt[2 * b + 0]), op=ALU.add)),
                r=[B1[1], k_(f"Rt{b}_0")], w=[k_(f"Rt{b}_1"), B1[1]])
            yield
            NNEWTON = 2
            for ns in range(NNEWTON):
                Rt, Rtk = (st.Rt[2 * b + 1], k_(f"Rt{b}_1")) if ns % 2 == 0 else (st.Rt[2 * b + 0], k_(f"Rt{b}_0"))
                if ns == NNEWTON - 1:
                    Rdst, Rdk = st.Rb, k_("Rb")
                else:
                    Rdst, Rdk = (st.Rt[2 * b + 0], k_(f"Rt{b}_0")) if ns % 2 == 0 else (st.Rt[2 * b + 1], k_(f"Rt{b}_1"))
                S.add("act", (lambda e, Rt=Rt: e.activation(
                    out=st.Rf.rearrange("p h c -> p (h c)"), in_=F(Rt),
                    func=AF.Identity)), r=[Rtk], w=[k_("Rf")])
                S.add("pool", (lambda e, Rt=Rt: e.tensor_tensor(
                    out=st.IR, in0=bcast(ident[0:64, 0:64].unsqueeze(1), [64, H, CH]), in1=Rt[:, 0:H, :],
                    op=ALU.subtract)), r=[Rtk, "consts"], w=[k_("IR")])
                yield
                for h in range(H):
                    S.add("pe", (lambda e, h=h: e.matmul(
                        B2[0][:, h * CH:(h + 1) * CH],
                        lhsT=X0f[:, h:h + 2, :].rearrange("p a c -> p (a c)"), rhs=st.Rf[:, h, :],
                        start=True, stop=True)), r=[X0k, k_("Rf")], w=[B2[1]])
                yield
                S.add("dve", (lambda e: e.scalar_tensor_tensor(
                    out=st.Eb.rearrange("p h c -> p (h c)"), in0=B2[0][0:64, :], scalar=-1.0,
                    in1=st.IR.rearrange("p h c -> p (h c)"), op0=ALU.mult, op1=ALU.add)),
                    r=[B2[1], k_("IR")], w=[k_("Eb"), B2[1]])
                yield
                for h in range(H):
                    S.add("pe", (lambda e, h=h, Rt=Rt: e.matmul(
                        B3[0][:, h * CH:(h + 1) * CH], lhsT=L2(Rt, h), rhs=identb,
                        start=True, stop=True)), r=[Rtk, "identb"], w=[B3[1]])
                yield
                S.add("act", (lambda e: e.activation(out=F(st.ETb),
                                                     in_=B3[0][0:64, :], func=AF.Identity)),
                      r=[B3[1]], w=[k_("ETb"), B3[1]])
                yield
                for h in range(H):
                    S.add("pe", (lambda e, h=h: e.matmul(
                        B1[0][:, h * CH:(h + 1) * CH], lhsT=L2(st.ETb, h), rhs=st.Eb[:, h, :],
                        start=True, stop=True)), r=[k_("ETb"), k_("Eb")], w=[B1[1]])
                yield
                S.add("dve", (lambda e, Rt=Rt, Rdst=Rdst: e.tensor_tensor(
                    out=F(Rdst), in0=B1[0][0:64, :], in1=F(Rt), op=ALU.add)),
                    r=[B1[1], Rtk], w=[Rdk, B1[1]])
                yield
            for h in range(H):
                bk = B2 if h < 4 else B3
                S.add("pe", (lambda e, h=h: e.matmul(
                    P1[0:64, h * DH:(h + 1) * DH], lhsT=st.Rb[:, h, :], rhs=st.vb[b][:, h, :],
                    start=True, stop=True)), r=[k_("Rb"), k_(f"vb{b}_")], w=[bk[1]])
            for h in range(H):
                S.add("pe", (lambda e, h=h: e.matmul(
                    B1[0][:, h * CH:(h + 1) * CH], lhsT=st.kbg[b][:, h, :], rhs=st.Rb[:, h, :],
                    start=True, stop=True)), r=[k_("Rb"), k_(f"kbg{b}_")], w=[B1[1]])
            yield
            S.add("act", (lambda e: e.activation(out=st.u[b].rearrange("p h c -> p (h c)"),
                                                 in_=P1[0:64, :], func=AF.Identity)),
                  r=[B2[1], B3[1]], w=[k_(f"u{b}_"), B2[1], B3[1]])
            S.add("dve", (lambda e: e.tensor_copy(out=st.wT[b].rearrange("p h c -> p (h c)"),
                                                  in_=B1[0])), r=[B1[1]], w=[k_(f"wT{b}_"), B1[1]])
            yield
            pdone[d] = step + 1

        for step, n in enumerate(order):
            yield from step_body(step, n)

    def g_recur(st, order):
        d = st.d
        B0 = pbank(d * 4)
        k_ = lambda nm: f"{nm}{d}"
        Sb = Sbf[:, d]
        Sf = S32[:, d]
        for step, n in enumerate(order):
            while pdone[d] <= step:
                yield "stall"
            b = step % 2
            is_lat = n >= 4
            u, wT, qd, qkt, kdec, dS = st.u[b], st.wT[b], st.qd[b], st.qkt[b], st.kdec[b], st.dS[b]
            uk, wTk, qdk, qktk, kdeck, dSk = (k_(f"{nm}{b}_") for nm in ("u", "wT", "qd", "qkt",
                                                                        "kdec", "dS"))
            for hf in range(2):
                for h4 in range(4):
                    h = hf * 4 + h4
                    S.add("pe", (lambda e, h=h, h4=h4, wT=wT: e.matmul(
                        B0[0][0:64, h4 * DH:(h4 + 1) * DH], lhsT=wT[:, h, :], rhs=Sb[:, h, :],
                        start=True, stop=True)), r=[wTk, f"Sbf_{d}"], w=[B0[1]])
                S.add("dve", (lambda e, hf=hf, u=u: e.tensor_tensor(
                    out=st.vnew[:, hf * 4:hf * 4 + 4, :].rearrange("p h c -> p (h c)"),
                    in0=u[:, hf * 4:hf * 4 + 4, :].rearrange("p h c -> p (h c)"),
                    in1=B0[0][0:64, :], op=ALU.subtract)),
                    r=[uk, B0[1]], w=[k_(f"vnew{hf}"), B0[1]])
                yield
            if is_lat:
                for h in range(H):
                    S.add("pe", (lambda e, h=h, qd=qd: e.matmul(
                        B0[0][:, h * CH:(h + 1) * CH], lhsT=Sb[:, h, :], rhs=qd[:, h, :],
                        start=True, stop=False)), r=[f"Sbf_{d}", qdk], w=[B0[1]])
                    S.add("pe", (lambda e, h=h, qkt=qkt: e.matmul(
                        B0[0][:, h * CH:(h + 1) * CH], lhsT=st.vnew[:, h, :], rhs=qkt[:, h, :],
                        start=False, stop=True)), r=[k_(f"vnew{h // 4}"), qktk], w=[B0[1]])
                ob = st.ost[b]
                obk = f"ost{d}_{b}"
                S.add("act", (lambda e, ob=ob: e.activation(out=ob.rearrange("p h c -> p (h c)"),
                                                            in_=B0[0], func=AF.Identity)),
                      r=[B0[1]], w=[obk, B0[1]])
                l0 = (n - 4) * CH
                S.dma("sp", (lambda e, ob=ob, l0=l0: e.dma_start(out=o_d[d, :, :, l0:l0 + CH],
                                                                 in_=ob)), r=[obk], w=[f"o_d{d}_{n}"])
                yield
            S.add("pool", (lambda e, dS=dS: e.tensor_tensor(
                out=Sf, in0=Sf, in1=bcast(dS.unsqueeze(2), [128, H, DH]), op=ALU.mult)),
                r=[f"S32_{d}", dSk], w=[f"S32_{d}"])
            for hf in range(2):
                for h4 in range(4):
                    h = hf * 4 + h4
                    S.add("pe", (lambda e, h=h, h4=h4, kdec=kdec: e.matmul(
                        B0[0][:, h4 * DH:(h4 + 1) * DH], lhsT=kdec[:, h, :], rhs=st.vnew[:, h, :],
                        start=True, stop=True)), r=[kdeck, k_(f"vnew{hf}")], w=[B0[1]])
                S.add("dve", (lambda e, hf=hf: e.tensor_tensor(
                    out=Sf[:, hf * 4:hf * 4 + 4, :].rearrange("p h c -> p (h c)"), in0=B0[0],
                    in1=Sf[:, hf * 4:hf * 4 + 4, :].rearrange("p h c -> p (h c)"), op=ALU.add)),
                    r=[f"S32_{d}", B0[1]], w=[f"S32_{d}", B0[1]])
                yield
            S.add("act", (lambda e: e.activation(out=Sb.rearrange("p h c -> p (h c)"),
                                                 in_=Sf.rearrange("p h c -> p (h c)"),
                                                 func=AF.Identity)),
                  r=[f"S32_{d}"], w=[f"Sbf_{d}"])
            rstep[d] = step + 1
            yield

    mG = A.mark()
    sts = [mk_stream(0), mk_stream(1)]
    order_f = list(range(NCH))
    order_b = [3, 2, 1, 0] + list(range(NCH - 1, 3, -1))
    if stop_after.startswith("G") and len(stop_after) > 1:
        nst = int(stop_after[1:])
        order_f, order_b = order_f[:nst], order_b[:nst]
    pdone = [0, 0]
    rstep = [0, 0]
    gens = [g_prep(sts[0], order_f), g_recur(sts[0], order_f),
            g_prep(sts[1], order_b), g_recur(sts[1], order_b)]
    alive = [True] * 4
    while any(alive):
        for gi, g in enumerate(gens):
            if alive[gi]:
                try:
                    next(g)
                except StopIteration:
                    alive[gi] = False
    A.reset(mG)
    A.reset(persist_mark)
    phase_barrier()
    if stop_after.startswith("G"):
        return nc, S, tail

    waT = A.t([128, 8, D], BF16, "waT")
    wbT = A.t([128, 8, D], BF16, "wbT")
    woT = A.t([128, 8, D], BF16, "woT")
    for (dst, src, key) in [(waT, wa_d, "waT"), (wbT, wb_d, "wbT"), (woT, wo_d, "woT")]:
        S.dma("pool", (lambda e, dst=dst, src=src: e.dma_start(
            out=dst, in_=src.rearrange("(kc p) n -> p kc n", p=128))), w=[key])
    ones_b = A.t([128, 128], BF16, "ones_b")
    S.add("dve", lambda e: e.tensor_copy(out=ones_b, in_=ones), r=["consts"], w=["ones_b"])
    ofh = [A.t([128, 512], F32, f"ofh{i}") for i in range(2)]
    obh = [A.t([128, 512], F32, f"obh{i}") for i in range(2)]
    zah = [A.t([128, 512], F32, f"zah{i}") for i in range(2)]
    sqh = [A.t([128, 512], BF16, f"sqh{i}") for i in range(2)]
    rsh = [A.t([128, 512], F32, f"rsh{i}") for i in range(2)]
    yaT = [A.t([128, H, 512], BF16, f"yaT{i}") for i in range(2)]
    ybT = [A.t([128, 8, 512], BF16, f"ybT{i}") for i in range(2)]
    mgT = [A.t([128, 8, 512], BF16, f"mgT{i}") for i in range(2)]
    sgt = [A.t([128, 2, 512], F32, f"sgt{i}") for i in range(2)]
    mt = [A.t([128, 2, 512], F32, f"mt{i}") for i in range(2)]
    xt2 = [A.t([128, D], F32, f"xt2_{i}") for i in range(2)]
    zt = [A.t([128, D], F32, f"zt_{i}") for i in range(2)]
    st2 = [A.t([128, 16], F32, f"st2_{i}") for i in range(2)]
    gate_row, lng_row, lnb_row = rows[:, 0:D], rows[:, D:2 * D], rows[:, 2 * D:3 * D]

    def m_taskA(bl):
        l0 = bl * 512
        bb = bl % 2
        S.dma("sp", (lambda e: e.dma_start(out=ybT[bb], in_=yb_d[:, :, l0:l0 + 512])),
              r=[f"yb_d{j}_{bl + 1}" for j in range(8)], w=[f"ybT{bb}"])
        def head_loads(h):
            hb = h % 2
            S.dma("sp", (lambda e, h=h, hb=hb: e.dma_start(out=ofh[hb],
                                                           in_=o_d[0, :, h, l0:l0 + 512])),
                  r=[f"o_d0_{4 + bl * 8 + i}" for i in range(8)], w=[f"ofh{hb}"])
            S.dma("sp", (lambda e, h=h, hb=hb: e.dma_start(out=obh[hb],
                                                           in_=o_d[1, :, h, l0:l0 + 512])),
                  r=[f"o_d1_{4 + bl * 8 + i}" for i in range(8)], w=[f"obh{hb}"])
            S.dma("sp", (lambda e, h=h, hb=hb: e.dma_start(out=zah[hb],
                                                           in_=za_d[:, h, l0:l0 + 512])),
                  r=[f"za_d{h}_{bl + 1}"], w=[f"zah{hb}"])

        def sg_loads(m):
            b = m % 2
            S.dma("sp", (lambda e, b=b, m=m: e.dma_start(
                out=sgt[b][:, 0, :], in_=sg_d[:, m, l0:l0 + 512])),
                r=[f"sg_d{m}_{bl + 1}"], w=[f"sgt{b}a"])
            S.dma("sp", (lambda e, b=b, m=m: e.dma_start(
                out=sgt[b][:, 1, :], in_=sg_d[:, 8 + m, l0:l0 + 512])),
                r=[f"sg_d{8 + m}_{bl + 1}"], w=[f"sgt{b}b"])

        head_loads(0)
        for h in range(H):
            hb = h % 2
            if h + 1 < H:
                head_loads(h + 1)
            else:
                sg_loads(0)
            S.add("dve", (lambda e, hb=hb: e.tensor_tensor(out=ofh[hb], in0=ofh[hb], in1=obh[hb],
                                                           op=ALU.add)),
                  r=[f"ofh{hb}", f"obh{hb}"], w=[f"ofh{hb}"])
            S.add("pool", (lambda e, hb=hb: e.tensor_tensor(out=sqh[hb], in0=ofh[hb], in1=ofh[hb],
                                                            op=ALU.mult)),
                  r=[f"ofh{hb}"], w=[f"sqh{hb}"])
            yield
            pb, pbk = pbank(hb)
            S.add("pe", (lambda e, hb=hb, pb=pb: e.matmul(pb, lhsT=ones_b, rhs=sqh[hb], start=True,
                                                          stop=True)),
                  r=[f"sqh{hb}", "ones_b"], w=[pbk])
            yield
            S.add("act", (lambda e, hb=hb, pb=pb: e.activation(out=rsh[hb], in_=pb, func=AF.Ln,
                                                               bias=RMS_EPS_AP, scale=1.0 / DH)),
                  r=[pbk, "consts"], w=[f"rsh{hb}", pbk])
            S.add("act", (lambda e, hb=hb: e.activation(out=rsh[hb], in_=rsh[hb], func=AF.Exp,
                                                        scale=-0.5)),
                  r=[f"rsh{hb}"], w=[f"rsh{hb}"])
            yield
            S.add("dve", (lambda e, hb=hb: e.tensor_tensor(out=ofh[hb], in0=ofh[hb], in1=rsh[hb],
                                                           op=ALU.mult)),
                  r=[f"ofh{hb}", f"rsh{hb}"], w=[f"ofh{hb}"])
            S.add("dve", (lambda e, h=h, hb=hb: e.scalar_tensor_tensor(
                out=yaT[bb][:, h, :], in0=ofh[hb], scalar=onw[:, 0:1], in1=zah[hb],
                op0=ALU.mult, op1=ALU.mult)), r=[f"ofh{hb}", f"zah{hb}", "onw"], w=[f"yaT{bb}"])
            yield
        for m in range(8):
            b = m % 2
            if m + 1 < 8:
                sg_loads(m + 1)
            pa, pak = pbank(2 + b * 2)
            pbb, pbbk = pbank(3 + b * 2)
            for j in range(8):
                S.add("pe", (lambda e, j=j, m=m, pa=pa: e.matmul(
                    pa, lhsT=waT[:, j, m * 128:(m + 1) * 128], rhs=yaT[bb][:, j, :],
                    start=(j == 0), stop=(j == 7))), r=["waT", f"yaT{bb}"], w=[pak], c=300.0)
            for j in range(8):
                S.add("pe", (lambda e, j=j, m=m, pbb=pbb: e.matmul(
                    pbb, lhsT=wbT[:, j, m * 128:(m + 1) * 128], rhs=ybT[bb][:, j, :],
                    start=(j == 0), stop=(j == 7))), r=["wbT", f"ybT{bb}"], w=[pbbk], c=300.0)
            yield
            S.add("dve", (lambda e, b=b, pa=pa: e.tensor_tensor(
                out=mt[b][:, 0, :], in0=pa, in1=sgt[b][:, 0, :], op=ALU.mult)),
                r=[pak, f"sgt{b}a"], w=[f"mt{b}a", pak])
            S.add("dve", (lambda e, b=b, pbb=pbb: e.tensor_tensor(
                out=mt[b][:, 1, :], in0=pbb, in1=sgt[b][:, 1, :], op=ALU.mult)),
                r=[pbbk, f"sgt{b}b"], w=[f"mt{b}b", pbbk])
            S.add("pool", (lambda e, b=b, m=m: e.tensor_tensor(
                out=mgT[bb][:, m, :], in0=mt[b][:, 0, :], in1=mt[b][:, 1, :], op=ALU.add)),
                r=[f"mt{b}a", f"mt{b}b"], w=[f"mgT{bb}"])
            yield

    mcnt = [0]
    pend_store = []

    def m_taskB(bl):
        l0 = bl * 512
        bb = bl % 2
        PY = PT[3]
        def x_load(sub, b):
            r0 = l0 + sub * 128
            S.dma("sp", (lambda e, b=b, r0=r0: e.dma_start(out=xt2[b], in_=x_d[r0:r0 + 128, :])),
                  w=[f"xt2_{b}"])

        def flush_store():
            if pend_store:
                b_, r0_ = pend_store.pop()
                op = S.dma("sp", (lambda e: e.dma_start(out=out_d[r0_:r0_ + 128, :], in_=xt2[b_])),
                           r=[f"xt2_{b_}"], w=["out_d"])
                tail.append(op)

        flush_store()
        x_load(0, mcnt[0] % 2)
        for sub in range(4):
            b = mcnt[0] % 2
            mcnt[0] += 1
            r0 = l0 + sub * 128
            flush_store()
            if sub + 1 < 4:
                x_load(sub + 1, mcnt[0] % 2)
            for half in range(2):
                pbk = f"ps{6 + half}"
                for m in range(8):
                    S.add("pe", (lambda e, m=m, half=half, sub=sub: e.matmul(
                        PY[:, half * 512:(half + 1) * 512],
                        lhsT=mgT[bb][:, m, sub * 128:(sub + 1) * 128],
                        rhs=woT[:, m, half * 512:(half + 1) * 512],
                        start=(m == 0), stop=(m == 7))), r=[f"mgT{bb}", "woT"], w=[pbk], c=300.0)
            yield
            S.add("dve", (lambda e, b=b: e.tensor_tensor(out=zt[b], in0=PY, in1=gate_row,
                                                         op=ALU.mult)),
                  r=["ps6", "ps7", "rows"], w=[f"zt_{b}", "ps6", "ps7"])
            S.add("dve", (lambda e, b=b: e.scalar_tensor_tensor(
                out=zt[b], in0=xt2[b], scalar=float(DN_ALPHA), in1=zt[b], op0=ALU.mult,
                op1=ALU.add)), r=[f"xt2_{b}", f"zt_{b}"], w=[f"zt_{b}"])
            yield
            st = st2[b]
            kst = f"st2_{b}"
            S.add("dve", (lambda e, b=b, st=st: e.bn_stats(out=st[:, 0:6], in_=zt[b][:, 0:512])),
                  r=[f"zt_{b}"], w=[kst + "a"])
            S.add("dve", (lambda e, b=b, st=st: e.bn_stats(out=st[:, 6:12], in_=zt[b][:, 512:1024])),
                  r=[f"zt_{b}"], w=[kst + "b"])
            S.add("dve", (lambda e, st=st: e.bn_aggr(out=st[:, 12:14], in_=st[:, 0:12])),
                  r=[kst + "a", kst + "b"], w=[kst + "mv"])
            S.add("act", (lambda e, st=st: e.activation(out=st[:, 14:15], in_=st[:, 13:14],
                                                        func=AF.Ln, bias=LN_EPS_AP, scale=1.0)),
                  r=[kst + "mv", "consts"], w=[kst + "r0"])
            S.add("act", (lambda e, st=st: e.activation(out=st[:, 14:15], in_=st[:, 14:15],
                                                        func=AF.Exp, scale=-0.5)),
                  r=[kst + "r0"], w=[kst + "r"])
            S.add("dve", (lambda e, st=st: e.scalar_tensor_tensor(
                out=st[:, 15:16], in0=st[:, 12:13], scalar=-1.0, in1=st[:, 14:15],
                op0=ALU.mult, op1=ALU.mult)), r=[kst + "mv", kst + "r"], w=[kst + "n"])
            yield
            S.add("act", (lambda e, b=b, st=st: e.activation(
                out=zt[b], in_=zt[b], func=AF.Identity, bias=st[:, 15:16], scale=st[:, 14:15])),
                r=[f"zt_{b}", kst + "r", kst + "n"], w=[f"zt_{b}"])
            yield
            S.add("pool", (lambda e, b=b: e.tensor_tensor(out=zt[b], in0=zt[b], in1=lng_row,
                                                          op=ALU.mult)),
                  r=[f"zt_{b}", "rows"], w=[f"zt_{b}"])
            S.add("pool", (lambda e, b=b: e.tensor_tensor(out=xt2[b], in0=zt[b], in1=lnb_row,
                                                          op=ALU.add)),
                  r=[f"zt_{b}", "rows"], w=[f"xt2_{b}"])
            pend_store.append((b, r0))
            yield

    def run_together(gens):
        alive = list(gens)
        while alive:
            for g in list(alive):
                try:
                    next(g)
                except StopIteration:
                    alive.remove(g)

    run_together([m_taskA(0)])
    for bl in range(1, 8):
        run_together([m_taskA(bl), m_taskB(bl - 1)])
    run_together([m_taskB(7)])
    if pend_store:
        b_, r0_ = pend_store.pop()
        tail.append(S.dma("sp", (lambda e: e.dma_start(out=out_d[r0_:r0_ + 128, :], in_=xt2[b_])),
                          r=[f"xt2_{b_}"], w=["out_d"]))
    return nc, S, tail


def _host_consts():
    c = np.zeros((128, 1024), np.float32)
    c[:, 0:128] = np.eye(128, dtype=np.float32)
    c[:, 128:256] = 1.0
    k = np.arange(64)[:, None]
    cc = np.arange(64)[None, :]
    c[0:64, 256:320] = (k <= cc)
    c[0:64, 320:384] = (k >= cc)
    r = np.arange(64)[:, None]
    q = np.arange(64)[None, :]
    m = [[(q < r), (q >= r)],
         [(q > r), (q <= r)]]
    for d in range(2):
        for w in range(2):
            o = 384 + (d * 2 + w) * 64
            c[0:64, o:o + 64] = np.where(m[d][w], 0.0, NEG)
    c[:, 640] = LN_EPS
    c[:, 641] = L2_EPS
    c[:, 642] = RMS_EPS
    c[:, 643] = 1.0
    return c


_CACHE = {}


def _get_program(debug=False, stop_after="M"):
    key = (debug, stop_after)
    if key not in _CACHE:
        nc, S, tail = build(debug, stop_after)
        S.finalize()
        S.emit(tail_ops=tail)
        _CACHE[key] = (nc, S)
    return _CACHE[key]


def _prep_inputs(x, c, ctx, c_ctx, w_mod, b_mod, w_in, b_in, conv_qkv_w, a_log, dt_bias,
                 o_norm_w, conv_b_w, conv_b_b, w_a, w_b, w_out, ln_g, ln_b):
    f = lambda a: np.ascontiguousarray(np.asarray(a, dtype=np.float32))
    x, c, ctx, c_ctx = f(x), f(c), f(ctx), f(c_ctx)
    w_mod, b_mod, w_in, b_in = f(w_mod)[0], f(b_mod)[0], f(w_in)[0], f(b_in)[0]
    cq, al, dtb = f(conv_qkv_w)[0], f(a_log)[0], f(dt_bias)[0]
    onw, cbw, cbb = f(o_norm_w)[0], f(conv_b_w)[0], f(conv_b_b)[0]
    w_a, w_b, w_out, ln_g, ln_b = f(w_a)[0], f(w_b)[0], f(w_out)[0], f(ln_g)[0], f(ln_b)[0]
    rep = lambda v: np.ascontiguousarray(np.broadcast_to(v[None, :], (128, v.shape[0])))
    bmod_fm = np.ascontiguousarray(b_mod.reshape(24, 128).T)
    rows = np.concatenate([rep(b_mod[2048:3072]), rep(ln_g), rep(ln_b)], axis=1)
    cols = np.concatenate([np.arange(0, 4096), np.arange(4128, IN_COLS)])
    bin_fm = np.ascontiguousarray(b_in[cols].reshape(80, 128).T)
    small = np.concatenate([rep(b_in[4096:4128]), rep(al.reshape(-1)), rep(dtb.reshape(-1))], axis=1)
    cqkv_fm = np.ascontiguousarray(cq.reshape(3, 24, 128).transpose(2, 1, 0).reshape(128, 72))
    cb4 = np.concatenate([cbw, cbb[None, :]], axis=0)
    cb_fm = np.ascontiguousarray(cb4.reshape(4, 8, 128).transpose(2, 1, 0).reshape(128, 32))
    common = {
        "w_mod": w_mod, "bmod_fm": bmod_fm, "rows": np.ascontiguousarray(rows), "w_in": w_in,
        "bin_fm": bin_fm, "small_rows": np.ascontiguousarray(small), "cqkv_fm": cqkv_fm,
        "cb_fm": cb_fm, "onw": np.ascontiguousarray(onw.reshape(128, 1)), "w_a": w_a, "w_b": w_b,
        "w_out": w_out, "consts": _host_consts(),
    }
    maps = []
    for b in range(x.shape[0]):
        cv = np.stack([c[b].reshape(8, 128).T, c_ctx.reshape(8, 128).T], axis=2)
        m = dict(common)
        m["x"] = x[b]
        m["ctx"] = ctx[b]
        m["cvec"] = np.ascontiguousarray(cv.reshape(128, 16))
        maps.append(m)
    return maps


def kernel(**inputs):
    maps = _prep_inputs(**inputs)
    nc, S = _get_program(False, "M")
    res = run_bass_kernel_spmd(nc, maps, core_ids=list(range(8)))
    return np.stack([np.asarray(r["out"], dtype=np.float32) for r in res.results], axis=0)
```

```python
import numpy as np
import concourse.bass as bass
import concourse.mybir as mybir
from concourse.bass_utils import run_bass_kernel_spmd

F32 = mybir.dt.float32
F32R = mybir.dt.float32r
BF16 = mybir.dt.bfloat16
ALU = mybir.AluOpType
AF = mybir.ActivationFunctionType

D = 1024
SEQ = 4096
CTX = 256
NTOK = SEQ + CTX
H = 8
DH = 128
CH = 64
NCH = NTOK // CH
IN_COLS = 10272
DN_ALPHA = 2.0 ** 0.25
LN_EPS = 1e-5
RMS_EPS = 1e-6
L2_EPS = 1e-6
NEG = -1.0e30

SEM_LIM = 30000
N_DMA_SEMS = 24


class Op:
    __slots__ = ("eng", "fn", "reads", "writes", "kind", "idx", "lidx", "signal",
                 "waits", "cnt", "dma_i", "is_bar", "cost", "grp", "done")

    def __init__(self, eng, fn, reads, writes, kind):
        self.eng = eng
        self.fn = fn
        self.reads = reads
        self.writes = writes
        self.kind = kind
        self.signal = False
        self.waits = []
        self.cnt = None
        self.dma_i = None
        self.is_bar = False
        self.cost = None
        self.grp = 0
        self.done = False


class Sched:
    ENGS = ("pe", "act", "dve", "pool", "sp")

    def __init__(self, nc):
        self.nc = nc
        self.ops = []

    def add(self, eng, fn, r=(), w=(), kind="c", c=None):
        op = Op(eng, fn, tuple(r), tuple(w), kind)
        op.cost = c
        op.idx = len(self.ops)
        if eng == "pe":
            if self.ops and self.ops[-1].eng == "pe":
                op.grp = self.ops[-1].grp
            else:
                self._ngrp = getattr(self, "_ngrp", 0) + 1
                op.grp = self._ngrp
        self.ops.append(op)
        return op

    def dma(self, eng, fn, r=(), w=()):
        return self.add(eng, fn, r, w, kind="d")

    def barrier(self, fn):
        op = self.add("pool", fn, (), ("__bar",))
        op.is_bar = True
        return op

    DEF_COST = {"pe": 70.0, "act": 600.0, "dve": 650.0, "pool": 1250.0, "sp": 60.0}

    def finalize(self, reorder=True):
        import heapq
        nc = self.nc
        ops = self.ops
        last_w = {}
        readers = {}
        deps_of = [None] * len(ops)
        seg_of = [0] * len(ops)
        seg = 0
        for op in ops:
            if op.is_bar:
                seg += 1
                seg_of[op.idx] = seg
                seg += 1
                deps_of[op.idx] = {}
                last_w = {}
                readers = {}
                continue
            seg_of[op.idx] = seg
            deps = {}
            for k in op.reads:
                p = last_w.get(k)
                if p is not None:
                    deps[p.idx] = p
            for k in op.writes:
                p = last_w.get(k)
                if p is not None:
                    deps[p.idx] = p
                for rd in readers.get(k, ()):
                    deps[rd.idx] = rd
            for k in op.reads:
                readers.setdefault(k, []).append(op)
            for k in op.writes:
                last_w[k] = op
                readers[k] = []
            deps.pop(op.idx, None)
            deps_of[op.idx] = deps
        nseg = seg + 1
        streams = {e: [] for e in self.ENGS}
        by_seg = [[] for _ in range(nseg)]
        for op in ops:
            by_seg[seg_of[op.idx]].append(op)
        DMA_LAT = 2500.0
        for sops in by_seg:
            if not sops:
                continue
            if len(sops) == 1 and sops[0].is_bar or not reorder:
                for op in sops:
                    streams[op.eng].append(op)
                continue
            indeg = {}
            succ = {}
            for op in sops:
                indeg[op.idx] = len(deps_of[op.idx])
                for p in deps_of[op.idx].values():
                    succ.setdefault(p.idx, []).append(op)
            fin = {}
            free = {e: 0.0 for e in self.ENGS}
            heaps = {e: [] for e in self.ENGS}
            pe_grp0 = {}
            for op in sops:
                if indeg[op.idx] == 0:
                    heapq.heappush(heaps[op.eng], (0.0, op.idx, op))
                    if op.eng == "pe":
                        heapq.heappush(pe_grp0.setdefault(op.grp, []), (op.idx, 0.0, op))
            left = len(sops)
            pe_grp = pe_grp0
            cur_grp = None
            while left:
                best = None
                for e in self.ENGS:
                    h_ = heaps[e]
                    while h_ and h_[0][2].done:
                        heapq.heappop(h_)
                    if h_:
                        rdy, idx, op = h_[0]
                        if e == "pe" and cur_grp is not None:
                            g_ = pe_grp.get(cur_grp)
                            while g_ and g_[0][2].done:
                                heapq.heappop(g_)
                            if g_ and g_[0][1] <= max(free[e], rdy) + 300.0:
                                idx, rdy, op = g_[0]
                        start = max(rdy, free[e])
                        if best is None or (start, idx) < (best[0], best[1]):
                            best = (start, idx, e, op, rdy)
                start, idx, e, op, rdy = best
                op.done = True
                if e == "pe":
                    cur_grp = op.grp
                cost = op.cost if op.cost is not None else self.DEF_COST[e]
                if op.kind == "d":
                    free[e] = start + 60.0
                    f = start + DMA_LAT + cost
                else:
                    free[e] = start + cost
                    f = free[e] + 60.0
                fin[idx] = f
                streams[e].append(op)
                left -= 1
                for q in succ.get(idx, ()):
                    indeg[q.idx] -= 1
                    if indeg[q.idx] == 0:
                        r = max(fin[p] for p in deps_of[q.idx])
                        heapq.heappush(heaps[q.eng], (r, q.idx, q))
                        if q.eng == "pe":
                            heapq.heappush(pe_grp.setdefault(q.grp, []), (q.idx, r, q))
        for e in self.ENGS:
            for i, op in enumerate(streams[e]):
                op.lidx = i
        cur_bar = None
        last_c = {}
        pend_d = []
        order = sorted(ops, key=lambda o: (seg_of[o.idx], o.idx))
        for op in order:
            if op.is_bar:
                for e in self.ENGS:
                    cands = [o for o in streams[e] if seg_of[o.idx] == seg_of[op.idx] - 1
                             and o.kind == "c"]
                    if cands:
                        p = cands[-1]
                        p.signal = True
                        op.waits.append(p)
                for p in pend_d:
                    op.waits.append(p)
                pend_d = []
                op.signal = True
                cur_bar = op
                continue
            if cur_bar is not None:
                op.waits.append(cur_bar)
            if op.kind == "d":
                pend_d.append(op)
            best_c = {}
            for p in deps_of[op.idx].values():
                if p.kind == "d":
                    op.waits.append(p)
                    continue
                if p.eng == op.eng and op.kind == "c":
                    if op.eng == "pe":
                        continue
                    if op.eng != "pool" and op.lidx - p.lidx > 3:
                        continue
                q = best_c.get(p.eng)
                if q is None or p.lidx > q.lidx:
                    best_c[p.eng] = p
            for p in best_c.values():
                p.signal = True
                op.waits.append(p)
        nsig = {e: 0 for e in self.ENGS}
        ndma = {e: 0 for e in self.ENGS}
        for e in self.ENGS:
            for op in streams[e]:
                if op.kind == "d":
                    op.dma_i = ndma[e]
                    ndma[e] += 1
                elif op.signal:
                    op.cnt = nsig[e]
                    nsig[e] += 1
        self.csems = {}
        self.dsems = {}
        for e in self.ENGS:
            n = (nsig[e] + SEM_LIM - 1) // SEM_LIM
            self.csems[e] = [nc.alloc_semaphore(name=f"c_{e}_{i}") for i in range(n)]
            if ndma[e]:
                self.dsems[e] = [nc.alloc_semaphore(name=f"d_{e}_{i}")
                                 for i in range(min(N_DMA_SEMS, ndma[e]))]
        self.streams = streams
        self.stats = dict(nsig=nsig, ndma=ndma,
                          nops={e: len(streams[e]) for e in self.ENGS})

    def _sem_val(self, p):
        if p.kind == "d":
            ns = len(self.dsems[p.eng])
            return self.dsems[p.eng][p.dma_i % ns], 16 * (p.dma_i // ns + 1)
        return self.csems[p.eng][p.cnt // SEM_LIM], (p.cnt % SEM_LIM) + 1

    def emit_engine(self, e, eng, tail_waits=()):
        waited = {}

        def do_wait(sem, val):
            key = sem.num
            if waited.get(key, 0) >= val:
                return
            waited[key] = val
            eng.wait_ge(sem, val)

        for op in self.streams[e]:
            for p in op.waits:
                sem, val = self._sem_val(p)
                do_wait(sem, val)
            if op.kind == "d":
                ns = len(self.dsems[e])
                if op.dma_i >= ns:
                    do_wait(self.dsems[e][op.dma_i % ns], 16 * (op.dma_i // ns))
                ins = op.fn(eng)
                sem, val = self._sem_val(op)
                ins.then_inc(sem, 16)
            else:
                ins = op.fn(eng)
                if op.signal:
                    sem, val = self._sem_val(op)
                    ins.then_inc(sem, 1)
        for p in tail_waits:
            sem, val = self._sem_val(p)
            do_wait(sem, val)

    def emit(self, tail_ops=()):
        nc = self.nc
        with nc.Block() as block:
            @block.tensor
            def _(eng):
                self.emit_engine("pe", eng)

            @block.scalar
            def _(eng):
                self.emit_engine("act", eng)

            @block.vector
            def _(eng):
                self.emit_engine("dve", eng)

            @block.gpsimd
            def _(eng):
                self.emit_engine("pool", eng)

            @block.sync
            def _(eng):
                self.emit_engine("sp", eng, tail_waits=tail_ops)


class Alloc:
    def __init__(self, nc, base=16512, limit=229300):
        self.nc = nc
        self.off = base
        self.limit = limit
        self.n = 0

    def t(self, shape, dtype, name=None):
        nbytes = int(np.prod(shape[1:])) * (2 if dtype == BF16 else 4)
        off = (self.off + 63) // 64 * 64
        assert off + nbytes <= self.limit, (name, off, nbytes, self.limit)
        self.off = off + nbytes
        self.n += 1
        h = self.nc.alloc_sbuf_tensor_at(f"{name or 't'}_{id(self) % 9973}_{self.n}", list(shape),
                                         dtype, offset=off)
        return h.ap()

    def mark(self):
        return self.off

    def reset(self, m):
        self.off = m


def bcast(ap, shape):
    return ap.broadcast_to(list(shape))


def build(debug=False, stop_after="M"):
    nc = bass.Bass("TRN2", target_bir_lowering=False)
    S = Sched(nc)

    def din(name, shape, dt=F32):
        return nc.dram_tensor(name, list(shape), dt, kind="ExternalInput").ap()

    def dscr(name, shape, dt=F32):
        kind = "ExternalOutput" if debug else "Internal"
        return nc.dram_tensor(name, list(shape), dt, kind=kind).ap()

    x_d = din("x", [SEQ, D])
    ctx_d = din("ctx", [CTX, D])
    cvec_d = din("cvec", [128, 16])
    wmod_d = din("w_mod", [D, 3 * D])
    bmodfm_d = din("bmod_fm", [128, 24])
    rows_d = din("rows", [128, 3 * D])
    win_d = din("w_in", [D, IN_COLS])
    binfm_d = din("bin_fm", [128, 80])
    small_d = din("small_rows", [128, 64])
    cqkv_d = din("cqkv_fm", [128, 72])
    cb_d = din("cb_fm", [128, 32])
    onw_d = din("onw", [128, 1])
    wa_d = din("w_a", [D, D])
    wb_d = din("w_b", [D, D])
    wo_d = din("w_out", [D, D])
    consts_d = din("consts", [128, 1024])
    out_d = nc.dram_tensor("out", [SEQ, D], F32, kind="ExternalOutput").ap()

    qT_d = dscr("qT_s", [128, NCH, H, CH], BF16)
    kT_d = dscr("kT_s", [128, NCH, H, CH], BF16)
    ktok_d = dscr("ktok_s", [NCH, CH, H, DH])
    vtok_d = dscr("vtok_s", [NCH, CH, H, DH])
    gb_d = dscr("gb_s", [NTOK, 32])
    za_d = dscr("za_s", [128, H, SEQ])
    yb_d = dscr("yb_s", [128, 8, SEQ], BF16)
    sg_d = dscr("sg_s", [128, 16, SEQ])
    o_d = dscr("o_s", [2, 128, H, SEQ])
    hx_dbg = dscr("hx_dbg", [128, 8, NTOK], BF16) if debug else None

    PT = [nc.alloc_psum_tensor(f"pt{i}", [128, 1024], F32).ap() for i in range(4)]

    def pbank(i):
        return PT[i // 2][:, (i % 2) * 512:(i % 2) * 512 + 512], f"ps{i}"

    A = Alloc(nc)
    consts = A.t([128, 1024], F32, "consts")
    ident = consts[:, 0:128]
    ones = consts[:, 128:256]
    tri = [consts[0:64, 256:320], consts[0:64, 320:384]]
    def mask_ap(d, which):
        o = 384 + (d * 2 + which) * 64
        return consts[0:64, o:o + 64]
    LN_EPS_AP = consts[:, 640:641]
    L2_EPS_AP = consts[:, 641:642]
    RMS_EPS_AP = consts[:, 642:643]
    ONE_AP = consts[:, 643:644]
    ZERO_AP = consts[:, 644:645]
    binfm = A.t([128, 80], F32, "binfm")
    small = A.t([128, 64], F32, "small")
    cqkv = A.t([128, 72], F32, "cqkv")
    cb = A.t([128, 32], F32, "cb")
    onw = A.t([128, 1], F32, "onw")
    cvec = A.t([128, 16], F32, "cvec")
    scv = A.t([128, 16], F32, "scv")
    bmodfm = A.t([128, 24], F32, "bmodfm")
    mod = A.t([128, 48], F32, "mod")
    sc1 = A.t([128, 16], F32, "sc1")
    negar = A.t([128, 16], F32, "negar")
    rows = A.t([128, 3 * D], F32, "rows")
    bar_t = A.t([128, 8], F32, "bar")
    persist_mark = A.mark()

    def phase_barrier():
        S.barrier(lambda e: e.memset(bar_t, 0.0))

    ld = []
    for (dst, src, key) in [(consts, consts_d, "consts"), (binfm, binfm_d, "binfm"),
                            (small, small_d, "small"), (cqkv, cqkv_d, "cqkv"), (cb, cb_d, "cb"),
                            (onw, onw_d, "onw"), (cvec, cvec_d, "cvec"),
                            (bmodfm, bmodfm_d, "bmodfm"), (rows, rows_d, "rows")]:
        S.dma("sp", (lambda e, dst=dst, src=src: e.dma_start(out=dst, in_=src)), w=[key])

    m0 = A.mark()
    wms = [A.t([128, 8, 1024], F32, f"wm{i}") for i in range(2)]
    screp = A.t([128, 8, 128], F32, "screp")
    wmod_r = wmod_d.rearrange("(kc p) n -> p kc n", p=128)
    S.add("act", lambda e: e.activation(out=scv, in_=cvec, func=AF.Silu), r=["cvec"], w=["scv"])
    scv3 = scv.rearrange("p (k j) -> p k j", j=2)
    S.add("pool", lambda e: e.tensor_copy(out=screp, in_=bcast(scv3[:, :, 0:1], [128, 8, 128])),
          r=["scv"], w=["screp"])
    S.add("act", lambda e: e.activation(out=negar, in_=small[:, 32:48], func=AF.Exp),
          r=["small"], w=["negar"])
    S.add("pool", lambda e: e.tensor_scalar(out=negar, in0=negar, scalar1=-1.0, scalar2=None,
                                            op0=ALU.mult), r=["negar"], w=["negar"])
    pm, pmk = pbank(0)
    pg0, pg0k = pbank(2)
    pg1, pg1k = pbank(3)
    for piece in range(3):
        wm, wmk = wms[piece % 2], f"wm{piece % 2}"
        S.dma("sp", (lambda e, piece=piece, wm=wm: e.dma_start(
            out=wm, in_=wmod_r[:, :, piece * 1024:(piece + 1) * 1024])), w=[wmk])
        for jj in range(8):
            j = piece * 8 + jj
            for kc in range(8):
                S.add("pe", (lambda e, j=j, jj=jj, kc=kc, wm=wm: e.matmul(
                    pm[:, j * 2:j * 2 + 2], lhsT=wm[:, kc, jj * 128:(jj + 1) * 128],
                    rhs=scv3[:, kc, :], start=(kc == 0), stop=(kc == 7))),
                    r=[wmk, "scv"], w=[pmk])
        if piece == 2:
            for half, (pg, pgk) in enumerate([(pg0, pg0k), (pg1, pg1k)]):
                for kc in range(8):
                    S.add("pe", (lambda e, half=half, kc=kc, pg=pg, wm=wm: e.matmul(
                        pg, lhsT=screp[:, kc, :], rhs=wm[:, kc, half * 512:(half + 1) * 512],
                        start=(kc == 0), stop=(kc == 7))), r=[wmk, "screp"], w=[pgk])
                S.add("dve", (lambda e, half=half, pg=pg: e.tensor_tensor(
                    out=rows[:, half * 512:(half + 1) * 512], in0=pg,
                    in1=rows[:, half * 512:(half + 1) * 512], op=ALU.add)),
                    r=[pgk, "rows"], w=["rows", pgk])
    mod3 = mod.rearrange("p (j t) -> p j t", t=2)
    S.add("dve", lambda e: e.tensor_tensor(
        out=mod3, in0=pm[:, 0:48].rearrange("p (j t) -> p j t", t=2),
        in1=bcast(bmodfm.unsqueeze(2), [128, 24, 2]), op=ALU.add),
        r=[pmk, "bmodfm"], w=["mod", pmk])
    S.add("dve", lambda e: e.tensor_scalar(out=sc1, in0=mod[:, 16:32], scalar1=1.0, scalar2=None,
                                           op0=ALU.add), r=["mod"], w=["sc1"])
    sc13 = sc1.rearrange("p (k j) -> p k j", j=2)
    A.reset(m0)
    phase_barrier()

    hxT = A.t([128, 8, NTOK], BF16, "hxT")
    mL = A.mark()
    NB = 4
    xt = [A.t([128, D], F32, f"xt{i}") for i in range(NB)]
    xn = [A.t([128, D], F32, f"xn{i}") for i in range(NB)]
    tmpL = [A.t([128, 512], F32, f"tmpL{i}") for i in range(2)]
    stats = [A.t([128, 16], F32, f"st{i}") for i in range(NB)]
    for i in range(NTOK // 128):
        b = i % NB
        jx = 1 if i < 2 else 0
        src = ctx_d[i * 128:(i + 1) * 128, :] if i < 2 else x_d[(i - 2) * 128:(i - 1) * 128, :]
        kx, kn, kst = f"xt{b}", f"xn{b}", f"st{b}"
        S.dma("sp", (lambda e, b=b, src=src: e.dma_start(out=xt[b], in_=src)), w=[kx])
        st = stats[b]
        S.add("dve", (lambda e, b=b, st=st: e.bn_stats(out=st[:, 0:6], in_=xt[b][:, 0:512])),
              r=[kx], w=[kst + "a"])
        S.add("dve", (lambda e, b=b, st=st: e.bn_stats(out=st[:, 6:12], in_=xt[b][:, 512:1024])),
              r=[kx], w=[kst + "b"])
        S.add("dve", (lambda e, st=st: e.bn_aggr(out=st[:, 12:14], in_=st[:, 0:12])),
              r=[kst + "a", kst + "b"], w=[kst + "mv"])
        S.add("act", (lambda e, st=st: e.activation(out=st[:, 14:15], in_=st[:, 13:14], func=AF.Ln,
                                                    bias=LN_EPS_AP, scale=1.0)),
              r=[kst + "mv", "consts"], w=[kst + "r0"])
        S.add("act", (lambda e, st=st: e.activation(out=st[:, 14:15], in_=st[:, 14:15], func=AF.Exp,
                                                    scale=-0.5)), r=[kst + "r0"], w=[kst + "r"])
        S.add("dve", (lambda e, st=st: e.scalar_tensor_tensor(
            out=st[:, 15:16], in0=st[:, 12:13], scalar=-1.0, in1=st[:, 14:15],
            op0=ALU.mult, op1=ALU.mult)), r=[kst + "mv", kst + "r"], w=[kst + "n"])
        S.add("act", (lambda e, b=b, st=st: e.activation(
            out=xn[b], in_=xt[b], func=AF.Identity, bias=st[:, 15:16], scale=st[:, 14:15])),
            r=[kx, kst + "r", kst + "n"], w=[kn])
        for half in range(2):
            pb, pbk = pbank(half)
            for q4 in range(4):
                kc = half * 4 + q4
                S.add("pe", (lambda e, b=b, kc=kc, q4=q4, pb=pb: e.transpose(
                    out=pb[:, q4 * 128:(q4 + 1) * 128], in_=xn[b][:, kc * 128:(kc + 1) * 128],
                    identity=ident)), r=[kn, "consts"], w=[pbk])
            tl, tlk = tmpL[half], f"tmpL{half}"
            S.add("dve", (lambda e, half=half, pb=pb, tl=tl, jx=jx: e.tensor_tensor(
                out=tl.rearrange("p (k t) -> p k t", t=128),
                in0=pb.rearrange("p (k t) -> p k t", t=128),
                in1=bcast(sc13[:, half * 4:half * 4 + 4, jx:jx + 1], [128, 4, 128]), op=ALU.mult)),
                r=[pbk, "sc1"], w=[tlk, pbk])
            S.add("pool", (lambda e, half=half, tl=tl, jx=jx, i=i: e.tensor_tensor(
                out=hxT[:, half * 4:half * 4 + 4, i * 128:(i + 1) * 128],
                in0=tl.rearrange("p (k t) -> p k t", t=128),
                in1=bcast(mod3[:, half * 4:half * 4 + 4, jx:jx + 1], [128, 4, 128]), op=ALU.add)),
                r=[tlk, "mod"], w=["hxT"])
    if debug:
        S.dma("sp", lambda e: e.dma_start(out=hx_dbg, in_=hxT), r=["hxT"], w=["hx_dbg"])
    A.reset(mL)
    phase_barrier()
    tail = []
    if stop_after == "L":
        return nc, S, tail

    mP = A.mark()
    NW = 8
    wbuf = [A.t([128, 8, 128], BF16, f"wb{i}") for i in range(NW)]
    wab = A.t([128, 8, 32], BF16, "wab")
    win_r = win_d.rearrange("(kc p) n -> p kc n", p=128)
    wslot = [0]

    def load_w(col0, width=128, dst=None):
        if dst is None:
            s = wslot[0] % NW
            wslot[0] += 1
            dst, key = wbuf[s], f"wb{s}"
        else:
            key = "wab"
        S.dma("pool", (lambda e, dst=dst, col0=col0, width=width: e.dma_start(
            out=dst, in_=win_r[:, :, col0:col0 + width])), w=[key])
        return dst, key

    NT = 4
    wt = [[A.t([128, 512], F32, f"w{s}_{i}") for i in range(6)] for s in range(NT)]
    wtb = [A.t([128, 512], BF16, f"wtb{s}") for s in range(NT)]
    wtt = [A.t([128, 512], F32, f"wtt{s}") for s in range(NT)]
    tset = [0]

    def blocks(with_ctx):
        res = []
        if with_ctx:
            res.append((0, 0, CTX, CTX))
        for bl in range(8):
            res.append((bl + 1, CTX + bl * 512, 512, 64))
        return res

    def proj(wk, w_ap, tok0, N, bank):
        pb, pbk = pbank(bank)
        for kc in range(8):
            S.add("pe", (lambda e, kc=kc, pb=pb, w_ap=w_ap, tok0=tok0, N=N: e.matmul(
                pb[:, 0:N], lhsT=w_ap[:, kc, :], rhs=hxT[:, kc, tok0:tok0 + N],
                start=(kc == 0), stop=(kc == 7))), r=[wk, "hxT"], w=[pbk], c=400.0)
        return pb, pbk

    def conv3(src, srck, dst, dstk, N, rowlen, w0, w1, w2, extra=None, cwk="cqkv"):
        if extra is None:
            S.add("dve", (lambda e: e.tensor_scalar(out=dst[:, 0:N], in0=src[:, 0:N], scalar1=w1,
                                                    scalar2=None, op0=ALU.mult)),
                  r=[srck, cwk], w=[dstk])
        else:
            S.add("dve", (lambda e: e.tensor_scalar(out=dst[:, 0:N], in0=src[:, 0:N], scalar1=w1,
                                                    scalar2=extra, op0=ALU.mult, op1=ALU.add)),
                  r=[srck, cwk], w=[dstk])
        s3 = src[:, 0:N].rearrange("p (r t) -> p r t", t=rowlen)
        d3 = dst[:, 0:N].rearrange("p (r t) -> p r t", t=rowlen)
        S.add("dve", (lambda e: e.scalar_tensor_tensor(
            out=d3[:, :, 1:rowlen], in0=s3[:, :, 0:rowlen - 1], scalar=w0, in1=d3[:, :, 1:rowlen],
            op0=ALU.mult, op1=ALU.add)), r=[srck, dstk, cwk], w=[dstk])
        S.add("dve", (lambda e: e.scalar_tensor_tensor(
            out=d3[:, :, 0:rowlen - 1], in0=s3[:, :, 1:rowlen], scalar=w2,
            in1=d3[:, :, 0:rowlen - 1], op0=ALU.mult, op1=ALU.add)), r=[srck, dstk, cwk],
            w=[dstk])

    nbinfm = A.t([128, 80], F32, "nbinfm")
    S.add("pool", lambda e: e.tensor_scalar(out=nbinfm, in0=binfm, scalar1=-1.0, scalar2=None,
                                            op0=ALU.mult), r=["binfm"], w=["nbinfm"])

    def sigmoid_chain(dst, dstk, src, srck, nbias, N):
        S.add("act", (lambda e: e.activation(out=dst[:, 0:N], in_=src[:, 0:N], func=AF.Exp,
                                             bias=nbias, scale=-1.0)),
              r=[srck, "nbinfm"], w=[dstk] + ([srck] if srck.startswith("ps") else []))
        S.add("act", (lambda e: e.activation(out=dst[:, 0:N], in_=dst[:, 0:N], func=AF.Ln,
                                             bias=ONE_AP, scale=1.0)), r=[dstk, "consts"], w=[dstk])
        S.add("act", (lambda e: e.activation(out=dst[:, 0:N], in_=dst[:, 0:N], func=AF.Exp,
                                             scale=-1.0)), r=[dstk], w=[dstk])

    class Grp:
        def __init__(self, cols):
            self.cols = cols
            self.w = None

        def ensure(self):
            if self.w is None:
                self.w = [load_w(c) for c in cols_of(self)]
            return self.w

    def cols_of(g):
        return g.cols

    groups = []

    def new_set():
        s = tset[0] % NT
        tset[0] += 1
        return s

    def qkv_block(gi, kind, h, blk):
        g = kind * 8 + h
        bl, tok0, N, rowlen = blk
        (w_ap, wk), = groups[gi].ensure()
        if gi + 1 < len(groups):
            groups[gi + 1].ensure()
        bias = binfm[:, g:g + 1]
        zero_b = ZERO_AP
        cw0, cw1, cw2 = (cqkv[:, g * 3 + i:g * 3 + i + 1] for i in range(3))
        s = new_set()
        T = wt[s]
        K = [f"w{s}_{i}" for i in range(6)]
        n0 = tok0 // CH
        nch = N // CH
        pb, pbk = proj(wk, w_ap, tok0, N, (0, 1, 6, 7)[s % 4])
        yield
        S.add("act", (lambda e: e.activation(out=T[0][:, 0:N], in_=pb[:, 0:N], func=AF.Identity,
                                             bias=bias, scale=1.0)),
              r=[pbk, "binfm"], w=[K[0], pbk])
        yield
        conv3(T[0], K[0], T[1], K[1], N, rowlen, cw0, cw1, cw2)
        yield
        sigmoid_chain(T[2], K[2], T[1], K[1], zero_b, N)
        yield
        S.add("dve", (lambda e: e.tensor_tensor(out=T[2][:, 0:N], in0=T[2][:, 0:N],
                                                in1=T[1][:, 0:N], op=ALU.mult)),
              r=[K[1], K[2]], w=[K[2]])
        fin, fink = T[2], K[2]
        if kind < 2:
            S.add("pool", (lambda e: e.tensor_tensor(out=T[3][:, 0:N], in0=T[2][:, 0:N],
                                                     in1=T[2][:, 0:N], op=ALU.mult)),
                  r=[K[2]], w=[K[3]])
            yield
            p2, p2k = pbank((2, 3, 4, 5)[s % 4])
            S.add("pe", (lambda e: e.matmul(p2[:, 0:N], lhsT=ones, rhs=T[3][:, 0:N], start=True,
                                            stop=True)), r=[K[3], "consts"], w=[p2k], c=1200.0)
            yield
            S.add("act", (lambda e: e.activation(out=T[4][:, 0:N], in_=p2[:, 0:N], func=AF.Ln,
                                                 bias=L2_EPS_AP, scale=1.0)),
                  r=[p2k, "consts"], w=[K[4], p2k])
            S.add("act", (lambda e: e.activation(out=T[4][:, 0:N], in_=T[4][:, 0:N], func=AF.Exp,
                                                 scale=-0.5)), r=[K[4]], w=[K[4]])
            yield
            qs = float(DH ** -0.5) if kind == 0 else 1.0
            if kind == 0:
                S.add("dve", (lambda e: e.scalar_tensor_tensor(
                    out=wtb[s][:, 0:N], in0=T[2][:, 0:N], scalar=qs, in1=T[4][:, 0:N],
                    op0=ALU.mult, op1=ALU.mult)), r=[K[2], K[4]], w=[f"wtb{s}"])
            else:
                S.add("dve", (lambda e: e.scalar_tensor_tensor(
                    out=T[5][:, 0:N], in0=T[2][:, 0:N], scalar=qs, in1=T[4][:, 0:N],
                    op0=ALU.mult, op1=ALU.mult)), r=[K[2], K[4]], w=[K[5]])
                fin, fink = T[5], K[5]
                S.add("pool", (lambda e: e.tensor_copy(out=wtb[s][:, 0:N], in_=T[5][:, 0:N])),
                      r=[K[5]], w=[f"wtb{s}"])
            dstT = (qT_d if kind == 0 else kT_d)
            S.dma("sp", (lambda e: e.dma_start(
                out=dstT[:, n0:n0 + nch, h, :],
                in_=wtb[s][:, 0:N].rearrange("p (n t) -> p n t", t=CH))),
                r=[f"wtb{s}"],
                w=[f"{'qT' if kind == 0 else 'kT'}_d{n}_{h}" for n in range(n0, n0 + nch)])
        if kind >= 1:
            yield
            nsub = N // 128
            p3, p3k = pbank((4, 5, 2, 3)[s % 4])
            for sub in range(nsub):
                S.add("pe", (lambda e, sub=sub: e.transpose(
                    out=p3[:, sub * 128:(sub + 1) * 128], in_=fin[:, sub * 128:(sub + 1) * 128],
                    identity=ident)), r=[fink, "consts"], w=[p3k])
            yield
            S.add("act", (lambda e: e.activation(out=wtt[s][:, 0:N], in_=p3[:, 0:N],
                                                 func=AF.Identity)), r=[p3k], w=[f"wtt{s}", p3k])
            dst = ktok_d if kind == 1 else vtok_d
            S.dma("sp", (lambda e: e.dma_start(
                out=dst[n0:n0 + nch, :, h, :].rearrange("(s n2) t d -> (n2 t) s d", s=nsub),
                in_=wtt[s][:, 0:N].rearrange("p (s d) -> p s d", d=128))),
                r=[f"wtt{s}"],
                w=[f"{'ktok' if kind == 1 else 'vtok'}_d{n}_{h}" for n in range(n0, n0 + nch)])

    def act_block(gi, bgrp, is_silu, dst_fn, dkey, blk):
        bl, tok0, N, rowlen = blk
        (w_ap, wk), = groups[gi].ensure()
        if gi + 1 < len(groups):
            groups[gi + 1].ensure()
        s = new_set()
        T = wt[s]
        pb, pbk = proj(wk, w_ap, tok0, N, (0, 1, 6, 7)[s % 4])
        yield
        sigmoid_chain(T[0], f"w{s}_0", pb, pbk, nbinfm[:, bgrp:bgrp + 1], N)
        if is_silu:
            S.add("dve", (lambda e: e.scalar_tensor_tensor(
                out=T[0], in0=pb, scalar=binfm[:, bgrp:bgrp + 1], in1=T[0], op0=ALU.add,
                op1=ALU.mult)), r=[pbk, f"w{s}_0", "binfm"], w=[f"w{s}_0", pbk])
        yield
        l0 = tok0 - CTX
        S.dma("sp", (lambda e: e.dma_start(out=dst_fn(l0), in_=T[0])),
              r=[f"w{s}_0"], w=[f"{dkey}_{bl}"])

    def bb_block(gi, j, blk):
        bl, tok0, N, rowlen = blk
        (wx, wxk), (wbg, wbgk), (wcg, wcgk), (wzb, wzbk) = groups[gi].ensure()
        if gi + 1 < len(groups):
            groups[gi + 1].ensure()
        bx, bbg, bcg, bzb = (binfm[:, 32 + 8 * t + j:32 + 8 * t + j + 1] for t in range(4))
        nbzb = nbinfm[:, 56 + j:56 + j + 1]
        c0, c1, c2, cbb = (cb[:, j * 4 + i:j * 4 + i + 1] for i in range(4))
        s = new_set()
        T = wt[s]
        K = [f"w{s}_{i}" for i in range(6)]
        pb0 = 4 * (s % 2)
        px, pxk = proj(wxk, wx, tok0, N, pb0 + 0)
        pc, pck = proj(wcgk, wcg, tok0, N, pb0 + 1)
        yield
        S.add("act", (lambda e: e.activation(out=T[0], in_=px, func=AF.Identity, bias=bx,
                                             scale=1.0)), r=[pxk, "binfm"], w=[K[0], pxk])
        pbg, pbgk = proj(wbgk, wbg, tok0, N, pb0 + 2)
        pz, pzk = proj(wzbk, wzb, tok0, N, pb0 + 3)
        yield
        S.add("dve", (lambda e: e.scalar_tensor_tensor(
            out=T[1], in0=pc, scalar=bcg, in1=T[0], op0=ALU.add, op1=ALU.mult)),
            r=[pck, K[0], "binfm"], w=[K[1], pck])
        sigmoid_chain(T[4], K[4], pz, pzk, nbzb, N)
        yield
        conv3(T[1], K[1], T[2], K[2], N, 64, c0, c1, c2, extra=cbb, cwk="cb")
        S.add("dve", (lambda e: e.scalar_tensor_tensor(
            out=T[4], in0=pz, scalar=bzb, in1=T[4], op0=ALU.add, op1=ALU.mult)),
            r=[pzk, K[4], "binfm"], w=[K[4], pzk])
        yield
        S.add("dve", (lambda e: e.scalar_tensor_tensor(
            out=T[3], in0=pbg, scalar=bbg, in1=T[2], op0=ALU.add, op1=ALU.mult)),
            r=[pbgk, K[2], "binfm"], w=[K[3], pbgk])
        yield
        S.add("pool", (lambda e: e.tensor_tensor(out=wtb[s], in0=T[3], in1=T[4], op=ALU.mult)),
              r=[K[3], K[4]], w=[f"wtb{s}"])
        l0 = tok0 - CTX
        S.dma("sp", (lambda e: e.dma_start(out=yb_d[:, j, l0:l0 + 512], in_=wtb[s])),
              r=[f"wtb{s}"], w=[f"yb_d{j}_{bl}"])

    ptasks = []
    for kind in range(3):
        for h in range(H):
            gi = len(groups)
            groups.append(Grp([(kind * 8 + h) * 128]))
            for blk in blocks(True):
                ptasks.append((lambda gi=gi, kind=kind, h=h, blk=blk: qkv_block(gi, kind, h, blk)))
    for h in range(H):
        gi = len(groups)
        groups.append(Grp([3072 + h * 128]))
        for blk in blocks(False):
            ptasks.append((lambda gi=gi, h=h, blk=blk: act_block(
                gi, 24 + h, True, (lambda l0, h=h: za_d[:, h, l0:l0 + 512]), f"za_d{h}", blk)))
    for m in range(16):
        gi = len(groups)
        groups.append(Grp([8224 + m * 128]))
        for blk in blocks(False):
            ptasks.append((lambda gi=gi, m=m, blk=blk: act_block(
                gi, 64 + m, False, (lambda l0, m=m: sg_d[:, m, l0:l0 + 512]), f"sg_d{m}", blk)))
    for j in range(8):
        gi = len(groups)
        groups.append(Grp([4128 + j * 128, 5152 + j * 128, 6176 + j * 128, 7200 + j * 128]))
        for blk in blocks(False):
            ptasks.append((lambda gi=gi, j=j, blk=blk: bb_block(gi, j, blk)))

    def run_pipelined(tasks, depth):
        active = []
        it = iter(tasks)
        while True:
            while len(active) < depth:
                t = next(it, None)
                if t is None:
                    break
                active.append(t())
            if not active:
                break
            for g in list(active):
                try:
                    next(g)
                except StopIteration:
                    active.remove(g)

    nbb = 8 * 8
    run_pipelined(ptasks[:-nbb], 3)
    if tset[0] % 2:
        tset[0] += 1
    run_pipelined(ptasks[-nbb:], 2)

    load_w(4096, 32, dst=wab)
    abt = [A.t([128, 32], F32, f"abt{i}") for i in range(2)]
    abx = [A.t([128, 16], F32, f"abx{i}") for i in range(2)]
    gbt = [A.t([128, 32], F32, f"gbt{i}") for i in range(2)]
    for i in range(NTOK // 128):
        b = i % 2
        pb, pbk = pbank(6 + b)
        for kc in range(8):
            S.add("pe", (lambda e, kc=kc, pb=pb, i=i: e.matmul(
                pb[:, 0:32], lhsT=hxT[:, kc, i * 128:(i + 1) * 128], rhs=wab[:, kc, :],
                start=(kc == 0), stop=(kc == 7))), r=["wab", "hxT"], w=[pbk])
        S.add("dve", (lambda e, pb=pb, b=b: e.tensor_tensor(out=abt[b], in0=pb[:, 0:32],
                                                            in1=small[:, 0:32], op=ALU.add)),
              r=[pbk, "small"], w=[f"abt{b}", pbk])
        S.add("dve", (lambda e, b=b: e.tensor_tensor(out=abx[b], in0=abt[b][:, 0:16],
                                                     in1=small[:, 48:64], op=ALU.add)),
              r=[f"abt{b}", "small"], w=[f"abx{b}"])
        S.add("act", (lambda e, b=b: e.activation(out=abx[b], in_=abx[b], func=AF.Exp)),
              r=[f"abx{b}"], w=[f"abx{b}"])
        S.add("act", (lambda e, b=b: e.activation(out=abx[b], in_=abx[b], func=AF.Ln, bias=ONE_AP,
                                                  scale=1.0)), r=[f"abx{b}", "consts"], w=[f"abx{b}"])
        S.add("dve", (lambda e, b=b: e.tensor_tensor(out=gbt[b][:, 0:16], in0=abx[b], in1=negar,
                                                     op=ALU.mult)),
              r=[f"abx{b}", "negar"], w=[f"gbt{b}a"])
        S.add("act", (lambda e, b=b: e.activation(out=gbt[b][:, 16:32], in_=abt[b][:, 16:32],
                                                  func=AF.Exp, scale=-1.0)),
              r=[f"abt{b}"], w=[f"gbt{b}b"])
        S.add("act", (lambda e, b=b: e.activation(out=gbt[b][:, 16:32], in_=gbt[b][:, 16:32],
                                                  func=AF.Ln, bias=ONE_AP, scale=1.0)),
              r=[f"gbt{b}b", "consts"], w=[f"gbt{b}b"])
        S.add("act", (lambda e, b=b: e.activation(out=gbt[b][:, 16:32], in_=gbt[b][:, 16:32],
                                                  func=AF.Exp, scale=-1.0)),
              r=[f"gbt{b}b"], w=[f"gbt{b}b"])
        S.dma("sp", (lambda e, b=b, i=i: e.dma_start(out=gb_d[i * 128:(i + 1) * 128, :],
                                                     in_=gbt[b])),
              r=[f"gbt{b}a", f"gbt{b}b"], w=[f"gb_d{i}"])
    A.reset(mP)
    A.reset(persist_mark)
    phase_barrier()
    if stop_after == "P":
        return nc, S, tail

    S32 = A.t([128, 2, H, DH], F32, "S32")
    Sbf = A.t([128, 2, H, DH], BF16, "Sbf")
    S.add("pool", lambda e: e.memset(S32, 0.0), w=["S32_0", "S32_1"])
    S.add("pool", lambda e: e.memset(Sbf, 0.0), w=["Sbf_0", "Sbf_1"])
    identR = A.t([64, CH], F32R, "identR")
    S.add("dve", lambda e: e.tensor_copy(out=identR, in_=ident[0:64, 0:64]), r=["consts"],
          w=["identR"])
    identb = A.t([64, CH], BF16, "identb")
    S.add("dve", lambda e: e.tensor_copy(out=identb, in_=ident[0:64, 0:64]), r=["consts"],
          w=["identb"])

    class St:
        pass

    def mk_stream(d):
        st = St()
        st.d = d
        st.kT = [A.t([128, H, CH], BF16, f"kT{d}_{i}") for i in range(2)]
        st.qT = [A.t([128, H, CH], BF16, f"qT{d}_{i}") for i in range(2)]
        st.ktok = [A.t([64, H, DH], F32, f"ktok{d}_{i}") for i in range(2)]
        st.vtok = [A.t([64, H, DH], F32, f"vtok{d}_{i}") for i in range(2)]
        st.gb = [A.t([64, 32], F32, f"gb{d}_{i}") for i in range(2)]
        st.trig = A.t([64, H, CH], F32, f"trig{d}")
        st.gct = A.t([64, 16], F32, f"gct{d}")
        st.t = A.t([64, 2, H, CH], F32, f"t{d}")
        st.E = A.t([64, 2, H, CH], F32, f"E{d}")
        st.kd = A.t([64, 8], F32, f"kd{d}")
        st.dS = [A.t([128, 8], F32, f"dS{d}_{i}") for i in range(2)]
        st.eqb = A.t([128, H, CH], F32, f"eqb{d}")
        st.qd = [A.t([128, H, CH], BF16, f"qd{d}_{i}") for i in range(2)]
        st.X0f = [A.t([64, H + 1, CH], F32R, f"X0f{d}_{i}") for i in range(2)]
        st.Xb = [A.t([64, H + 1, CH], BF16, f"Xb{d}_{i}") for i in range(2)]
        st.Yb = [A.t([64, H + 1, CH], BF16, f"Yb{d}_{i}") for i in range(2)]
        st.Rt = [A.t([64, H + 1, CH], BF16, f"Rt{d}_{i}") for i in range(4)]
        st.Rf = A.t([64, H, CH], F32R, f"Rf{d}")
        st.IR = A.t([64, H, CH], F32, f"IR{d}")
        st.Eb = A.t([64, H, CH], BF16, f"Eb{d}")
        st.ETb = A.t([64, H + 1, CH], BF16, f"ETb{d}")
        st.Rb = A.t([64, H, CH], BF16, f"Rb{d}")
        st.vb = [A.t([64, H, DH], BF16, f"vb{d}_{i}") for i in range(2)]
        st.kbg = [A.t([64, H, DH], BF16, f"kbg{d}_{i}") for i in range(2)]
        st.kdec = [A.t([64, H, DH], BF16, f"kdec{d}_{i}") for i in range(2)]
        st.qkt = [A.t([64, H, CH], BF16, f"qkt{d}_{i}") for i in range(2)]
        st.u = [A.t([64, H, DH], F32, f"u{d}_{i}") for i in range(2)]
        st.wT = [A.t([128, H, CH], BF16, f"wT{d}_{i}") for i in range(2)]
        st.vnew = A.t([64, H, DH], BF16, f"vnew{d}")
        st.ost = [A.t([128, H, CH], F32, f"ost{d}_{i}") for i in range(2)]
        return st

    def g_prep(st, order):
        d = st.d
        B0, B1, B2, B3 = (pbank(d * 4 + i) for i in range(4))
        P1 = PT[d * 2 + 1]
        k_ = lambda nm: f"{nm}{d}"

        def L2(t, h):
            return t[:, h:h + 2, :].rearrange("p a c -> p (a c)")

        def F(t):
            return t[:, 0:H, :].rearrange("p h c -> p (h c)")
        last = CH - 1 if d == 0 else 0
        def issue_loads(step):
            n = order[step]
            b = step % 2
            is_lat = n >= 4
            kTk, qTk, ktokk, vtokk, gbk = (f"{nm}{d}_{b}" for nm in ("kT", "qT", "ktok", "vtok",
                                                                     "gb"))
            kT, qT, ktok, vtok, gb = st.kT[b], st.qT[b], st.ktok[b], st.vtok[b], st.gb[b]
            S.dma("sp", (lambda e, kT=kT, n=n: e.dma_start(out=kT, in_=kT_d[:, n, :, :])),
                  r=[f"kT_d{n}_{h}" for h in range(H)], w=[kTk])
            if is_lat:
                S.dma("sp", (lambda e, qT=qT, n=n: e.dma_start(out=qT, in_=qT_d[:, n, :, :])),
                      r=[f"qT_d{n}_{h}" for h in range(H)], w=[qTk])
            S.dma("sp", (lambda e, ktok=ktok, n=n: e.dma_start(out=ktok, in_=ktok_d[n])),
                  r=[f"ktok_d{n}_{h}" for h in range(H)], w=[ktokk])
            S.dma("sp", (lambda e, vtok=vtok, n=n: e.dma_start(out=vtok, in_=vtok_d[n])),
                  r=[f"vtok_d{n}_{h}" for h in range(H)], w=[vtokk])
            S.dma("sp", (lambda e, gb=gb, n=n: e.dma_start(out=gb, in_=gb_d[n * CH:(n + 1) * CH, :])),
                  r=[f"gb_d{n // 2}"], w=[gbk])

        issue_loads(0)

        def step_body(step, n):
            while rstep[d] < step - 1:
                yield "stall"
            b = step % 2
            is_lat = n >= 4
            kTk, qTk, ktokk, vtokk, gbk = (f"{nm}{d}_{b}" for nm in ("kT", "qT", "ktok", "vtok",
                                                                     "gb"))
            kT, qT, ktok, vtok, gb = st.kT[b], st.qT[b], st.ktok[b], st.vtok[b], st.gb[b]
            if step + 1 < len(order):
                issue_loads(step + 1)
            g_ap = gb[:, d * 8:d * 8 + 8]
            beta_ap = gb[:, 16 + d * 8:16 + d * 8 + 8]
            yield
            S.add("pool", (lambda e, g_ap=g_ap: e.tensor_tensor(
                out=st.trig, in0=bcast(tri[d].unsqueeze(1), [64, H, CH]),
                in1=bcast(g_ap.unsqueeze(2), [64, H, CH]), op=ALU.mult)),
                r=[gbk, "consts"], w=[k_("trig")])
            S.add("pe", (lambda e, g_ap=g_ap: e.matmul(B0[0][0:64, 0:8], lhsT=tri[d], rhs=g_ap,
                                                       start=True, stop=True)),
                  r=[gbk, "consts"], w=[B0[1]])
            S.add("act", (lambda e: e.activation(out=st.gct[:, 0:8], in_=B0[0][0:64, 0:8],
                                                 func=AF.Identity)),
                  r=[B0[1]], w=[k_("gct"), B0[1]])
            S.add("pe", (lambda e: e.matmul(B0[0], lhsT=ones[0:64, :],
                                            rhs=st.trig.rearrange("p h c -> p (h c)"),
                                            start=True, stop=True)),
                  r=[k_("trig"), "consts"], w=[B0[1]], c=950.0)
            gcb3 = B0[0][0:64, :].rearrange("p (h c) -> p h c", c=CH)
            gcbf = B0[0].rearrange("p (h c) -> p h c", c=CH)
            gct_b = bcast(st.gct[:, 0:8].unsqueeze(2), [64, H, CH])
            S.add("dve", (lambda e: e.scalar_tensor_tensor(
                out=st.t[:, 0], in0=gcb3, scalar=-1.0, in1=gct_b, op0=ALU.mult, op1=ALU.add)),
                r=[B0[1], k_("gct")], w=[k_("t0"), B0[1]])
            S.add("dve", (lambda e: e.scalar_tensor_tensor(
                out=st.t[:, 1], in0=gct_b, scalar=-1.0, in1=gcb3, op0=ALU.mult, op1=ALU.add)),
                r=[B0[1], k_("gct")], w=[k_("t1"), B0[1]])
            for wh in range(2):
                S.add("pool", (lambda e, wh=wh: e.tensor_tensor(
                    out=st.t[:, wh], in0=st.t[:, wh],
                    in1=bcast(mask_ap(d, wh).unsqueeze(1), [64, H, CH]), op=ALU.add)),
                    r=[k_(f"t{wh}"), "consts"], w=[k_(f"t{wh}")])
            S.add("act", (lambda e: e.activation(out=st.E.rearrange("p a h c -> p (a h c)"),
                                                 in_=st.t.rearrange("p a h c -> p (a h c)"),
                                                 func=AF.Exp)),
                  r=[k_("t0"), k_("t1")], w=[k_("E0"), k_("E1")])
            S.add("pool", (lambda e, beta_ap=beta_ap: e.tensor_tensor(
                out=st.E[:, 0], in0=st.E[:, 0], in1=bcast(beta_ap.unsqueeze(2), [64, H, CH]),
                op=ALU.mult)), r=[k_("E0"), gbk], w=[k_("E0")])
            S.add("dve", (lambda e: e.tensor_tensor(out=st.kd, in0=gcb3[:, :, last],
                                                    in1=st.gct[:, 0:8], op=ALU.subtract)),
                  r=[B0[1], k_("gct")], w=[k_("kd"), B0[1]])
            S.add("act", (lambda e: e.activation(out=st.kd, in_=st.kd, func=AF.Exp)),
                  r=[k_("kd")], w=[k_("kd")])
            S.add("act", (lambda e: e.activation(out=st.gct[:, 8:16], in_=st.gct[:, 0:8],
                                                 func=AF.Exp)), r=[k_("gct")], w=[k_("coef")])
            S.add("pool", (lambda e, beta_ap=beta_ap: e.tensor_tensor(
                out=st.gct[:, 8:16], in0=st.gct[:, 8:16], in1=beta_ap, op=ALU.mult)),
                r=[k_("coef"), gbk], w=[k_("coef")])
            S.add("act", (lambda e: e.activation(out=st.dS[b], in_=gcbf[:, :, last], func=AF.Exp)),
                  r=[B0[1]], w=[k_(f"dS{b}_"), B0[1]])
            if is_lat:
                S.add("act", (lambda e: e.activation(out=st.eqb.rearrange("p h c -> p (h c)"),
                                                     in_=B0[0], func=AF.Exp)),
                      r=[B0[1]], w=[k_("eqb"), B0[1]])
                S.add("pool", (lambda e, qT=qT: e.tensor_tensor(out=st.qd[b], in0=qT, in1=st.eqb,
                                                                op=ALU.mult)),
                      r=[qTk, k_("eqb")], w=[k_(f"qd{b}_")])
            S.add("pool", (lambda e, vtok=vtok, beta_ap=beta_ap: e.tensor_tensor(
                out=st.vb[b], in0=vtok, in1=bcast(beta_ap.unsqueeze(2), [64, H, DH]), op=ALU.mult)),
                r=[vtokk, gbk], w=[k_(f"vb{b}_")])
            S.add("pool", (lambda e, ktok=ktok: e.tensor_tensor(
                out=st.kbg[b], in0=ktok, in1=bcast(st.gct[:, 8:16].unsqueeze(2), [64, H, DH]),
                op=ALU.mult)), r=[ktokk, k_("coef")], w=[k_(f"kbg{b}_")])
            S.add("pool", (lambda e, ktok=ktok: e.tensor_tensor(
                out=st.kdec[b], in0=ktok, in1=bcast(st.kd.unsqueeze(2), [64, H, DH]), op=ALU.mult)),
                r=[ktokk, k_("kd")], w=[k_(f"kdec{b}_")])
            yield
            for h in range(H):
                S.add("pe", (lambda e, h=h, kT=kT: e.matmul(
                    B2[0][0:64, h * CH:(h + 1) * CH], lhsT=kT[:, h, :], rhs=kT[:, h, :],
                    start=True, stop=True)), r=[kTk], w=[B2[1]])
            if is_lat:
                for h in range(H):
                    S.add("pe", (lambda e, h=h, kT=kT, qT=qT: e.matmul(
                        B3[0][0:64, h * CH:(h + 1) * CH], lhsT=kT[:, h, :], rhs=qT[:, h, :],
                        start=True, stop=True)), r=[kTk, qTk], w=[B3[1]])
            yield
            X0f, X0k = st.X0f[b], k_(f"X0f{b}_")
            S.add("dve", (lambda e: e.tensor_tensor(
                out=X0f[:, 0:H, :].rearrange("p h c -> p (h c)"), in0=B2[0][0:64, :],
                in1=st.E[:, 0].rearrange("p h c -> p (h c)"), op=ALU.mult)),
                r=[B2[1], k_("E0")], w=[X0k, B2[1]])
            if is_lat:
                S.add("dve", (lambda e: e.tensor_tensor(
                    out=st.qkt[b].rearrange("p h c -> p (h c)"), in0=B3[0][0:64, :],
                    in1=st.E[:, 1].rearrange("p h c -> p (h c)"), op=ALU.mult)),
                    r=[B3[1], k_("E1")], w=[k_(f"qkt{b}_"), B3[1]])
            S.add("act", (lambda e: e.activation(out=F(st.Xb[0]),
                                                 in_=X0f[:, 0:H, :].rearrange("p h c -> p (h c)"),
                                                 func=AF.Identity)), r=[X0k], w=[k_("Xb0")])
            yield
            for h in range(H):
                S.add("pe", (lambda e, h=h: e.matmul(
                    B1[0][:, h * CH:(h + 1) * CH], lhsT=L2(st.Xb[0], h), rhs=identb,
                    start=True, stop=True)), r=[k_("Xb0"), "identb"], w=[B1[1]])
            yield
            S.add("act", (lambda e: e.activation(out=F(st.Yb[0]),
                                                 in_=B1[0][0:64, :], func=AF.Identity)),
                  r=[B1[1]], w=[k_("Yb0"), B1[1]])
            S.add("dve", (lambda e: e.scalar_tensor_tensor(
                out=st.Rt[2 * b + 0][:, 0:H, :], in0=B1[0][0:64, :].rearrange("p (h c) -> p h c", c=CH), scalar=-1.0,
                in1=bcast(ident[0:64, 0:64].unsqueeze(1), [64, H, CH]),
                op0=ALU.mult, op1=ALU.add)), r=[B1[1], "consts"], w=[k_(f"Rt{b}_0"), B1[1]])
            yield
            rp = 0
            for lvl in range(1, 6):
                pi, ci = (lvl - 1) % 2, lvl % 2
                Xp, Yp = st.Xb[pi], st.Yb[pi]
                Xn, Yn = st.Xb[ci], st.Yb[ci]
                Xpk, Ypk = k_(f"Xb{pi}"), k_(f"Yb{pi}")
                Xnk, Ynk = k_(f"Xb{ci}"), k_(f"Yb{ci}")
                for h in range(H):
                    S.add("pe", (lambda e, h=h, Xp=Xp, Yp=Yp: e.matmul(
                        B2[0][:, h * CH:(h + 1) * CH], lhsT=L2(Yp, h), rhs=Xp[:, h, :],
                        start=True, stop=True)), r=[Xpk, Ypk], w=[B2[1]])
                if lvl < 5:
                    for h in range(H):
                        S.add("pe", (lambda e, h=h, Xp=Xp, Yp=Yp: e.matmul(
                            B3[0][:, h * CH:(h + 1) * CH], lhsT=L2(Xp, h), rhs=Yp[:, h, :],
                            start=True, stop=True)), r=[Xpk, Ypk], w=[B3[1]])
                if lvl >= 2:
                    Rp, Rpk = st.Rt[2 * b + rp], k_(f"Rt{b}_{rp}")
                    for h in range(H):
                        S.add("pe", (lambda e, h=h, Xp=Xp, Rp=Rp: e.matmul(
                            B1[0][:, h * CH:(h + 1) * CH], lhsT=L2(Xp, h), rhs=Rp[:, h, :],
                            start=True, stop=True)), r=[Xpk, Rpk], w=[B1[1]])
                yield
                S.add("act", (lambda e, Xn=Xn: e.activation(out=F(Xn),
                                                            in_=B2[0][0:64, :], func=AF.Identity)),
                      r=[B2[1]], w=[Xnk, B2[1]])
                if lvl < 5:
                    S.add("dve", (lambda e, Yn=Yn: e.tensor_copy(
                        out=F(Yn), in_=B3[0][0:64, :])),
                        r=[B3[1]], w=[Ynk, B3[1]])
                if lvl >= 2:
                    Rn, Rnk = st.Rt[2 * b + 1 - rp], k_(f"Rt{b}_{1 - rp}")
                    S.add("dve", (lambda e, Rn=Rn, Rp=Rp: e.tensor_tensor(
                        out=F(Rn), in0=B1[0][0:64, :], in1=F(Rp), op=ALU.add)),
                        r=[B1[1], Rpk], w=[Rnk, B1[1]])
                    rp = 1 - rp
                yield
            assert rp == 0
            for h in range(H):
                S.add("pe", (lambda e, h=h: e.matmul(
                    B1[0][:, h * CH:(h + 1) * CH], lhsT=L2(st.Xb[1], h), rhs=st.Rt[2 * b + 0][:, h, :],
                    start=True, stop=True)), r=[k_("Xb1"), k_(f"Rt{b}_0")], w=[B1[1]])
            yield
            S.add("dve", (lambda e: e.tensor_tensor(
                out=F(st.Rt[2 * b + 1]), in0=B1[0][0:64, :], in1=F(st.Rt[2 * b + 0]), op=ALU.add)),
                r=[B1[1], k_(f"Rt{b}_0")], w=[k_(f"Rt{b}_1"), B1[1]])
            yield
            NNEWTON = 2
            for ns in range(NNEWTON):
                Rt, Rtk = (st.Rt[2 * b + 1], k_(f"Rt{b}_1")) if ns % 2 == 0 else (st.Rt[2 * b + 0], k_(f"Rt{b}_0"))
                if ns == NNEWTON - 1:
                    Rdst, Rdk = st.Rb, k_("Rb")
                else:
                    Rdst, Rdk = (st.Rt[2 * b + 0], k_(f"Rt{b}_0")) if ns % 2 == 0 else (st.Rt[2 * b + 1], k_(f"Rt{b}_1"))
                S.add("act", (lambda e, Rt=Rt: e.activation(
                    out=st.Rf.rearrange("p h c -> p (h c)"), in_=F(Rt),
                    func=AF.Identity)), r=[Rtk], w=[k_("Rf")])
                S.add("pool", (lambda e, Rt=Rt: e.tensor_tensor(
                    out=st.IR, in0=bcast(ident[0:64, 0:64].unsqueeze(1), [64, H, CH]), in1=Rt[:, 0:H, :],
                    op=ALU.subtract)), r=[Rtk, "consts"], w=[k_("IR")])
                yield
                for h in range(H):
                    S.add("pe", (lambda e, h=h: e.matmul(
                        B2[0][:, h * CH:(h + 1) * CH],
                        lhsT=X0f[:, h:h + 2, :].rearrange("p a c -> p (a c)"), rhs=st.Rf[:, h, :],
                        start=True, stop=True)), r=[X0k, k_("Rf")], w=[B2[1]])
                yield
                S.add("dve", (lambda e: e.scalar_tensor_tensor(
                    out=st.Eb.rearrange("p h c -> p (h c)"), in0=B2[0][0:64, :], scalar=-1.0,
                    in1=st.IR.rearrange("p h c -> p (h c)"), op0=ALU.mult, op1=ALU.add)),
                    r=[B2[1], k_("IR")], w=[k_("Eb"), B2[1]])
                yield
                for h in range(H):
                    S.add("pe", (lambda e, h=h, Rt=Rt: e.matmul(
                        B3[0][:, h * CH:(h + 1) * CH], lhsT=L2(Rt, h), rhs=identb,
                        start=True, stop=True)), r=[Rtk, "identb"], w=[B3[1]])
                yield
                S.add("act", (lambda e: e.activation(out=F(st.ETb),
                                                     in_=B3[0][0:64, :], func=AF.Identity)),
                      r=[B3[1]], w=[k_("ETb"), B3[1]])
                yield
                for h in range(H):
                    S.add("pe", (lambda e, h=h: e.matmul(
                        B1[0][:, h * CH:(h + 1) * CH], lhsT=L2(st.ETb, h), rhs=st.Eb[:, h, :],
                        start=True, stop=True)), r=[k_("ETb"), k_("Eb")], w=[B1[1]])
                yield
                S.add("dve", (lambda e, Rt=Rt, Rdst=Rdst: e.tensor_tensor(
                    out=F(Rdst), in0=B1[0][0:64, :], in1=F(Rt), op=ALU.add)),
                    r=[B1[1], Rtk], w=[Rdk, B1[1]])
                yield
            for h in range(H):
                bk = B2 if h < 4 else B3
                S.add("pe", (lambda e, h=h: e.matmul(
                    P1[0:64, h * DH:(h + 1) * DH], lhsT=st.Rb[:, h, :], rhs=st.vb[b][:, h, :],
                    start=True, stop=True)), r=[k_("Rb"), k_(f"vb{b}_")], w=[bk[1]])
            for h in range(H):
                S.add("pe", (lambda e, h=h: e.matmul(
                    B1[0][:, h * CH:(h + 1) * CH], lhsT=st.kbg[b][:, h, :], rhs=st.Rb[:, h, :],
                    start=True, stop=True)), r=[k_("Rb"), k_(f"kbg{b}_")], w=[B1[1]])
            yield
            S.add("act", (lambda e: e.activation(out=st.u[b].rearrange("p h c -> p (h c)"),
                                                 in_=P1[0:64, :], func=AF.Identity)),
                  r=[B2[1], B3[1]], w=[k_(f"u{b}_"), B2[1], B3[1]])
            S.add("dve", (lambda e: e.tensor_copy(out=st.wT[b].rearrange("p h c -> p (h c)"),
                                                  in_=B1[0])), r=[B1[1]], w=[k_(f"wT{b}_"), B1[1]])
            yield
            pdone[d] = step + 1

        for step, n in enumerate(order):
            yield from step_body(step, n)

    def g_recur(st, order):
        d = st.d
        B0 = pbank(d * 4)
        k_ = lambda nm: f"{nm}{d}"
        Sb = Sbf[:, d]
        Sf = S32[:, d]
        for step, n in enumerate(order):
            while pdone[d] <= step:
                yield "stall"
            b = step % 2
            is_lat = n >= 4
            u, wT, qd, qkt, kdec, dS = st.u[b], st.wT[b], st.qd[b], st.qkt[b], st.kdec[b], st.dS[b]
            uk, wTk, qdk, qktk, kdeck, dSk = (k_(f"{nm}{b}_") for nm in ("u", "wT", "qd", "qkt",
                                                                        "kdec", "dS"))
            for hf in range(2):
                for h4 in range(4):
                    h = hf * 4 + h4
                    S.add("pe", (lambda e, h=h, h4=h4, wT=wT: e.matmul(
                        B0[0][0:64, h4 * DH:(h4 + 1) * DH], lhsT=wT[:, h, :], rhs=Sb[:, h, :],
                        start=True, stop=True)), r=[wTk, f"Sbf_{d}"], w=[B0[1]])
                S.add("dve", (lambda e, hf=hf, u=u: e.tensor_tensor(
                    out=st.vnew[:, hf * 4:hf * 4 + 4, :].rearrange("p h c -> p (h c)"),
                    in0=u[:, hf * 4:hf * 4 + 4, :].rearrange("p h c -> p (h c)"),
                    in1=B0[0][0:64, :], op=ALU.subtract)),
                    r=[uk, B0[1]], w=[k_(f"vnew{hf}"), B0[1]])
                yield
            if is_lat:
                for h in range(H):
                    S.add("pe", (lambda e, h=h, qd=qd: e.matmul(
                        B0[0][:, h * CH:(h + 1) * CH], lhsT=Sb[:, h, :], rhs=qd[:, h, :],
                        start=True, stop=False)), r=[f"Sbf_{d}", qdk], w=[B0[1]])
                    S.add("pe", (lambda e, h=h, qkt=qkt: e.matmul(
                        B0[0][:, h * CH:(h + 1) * CH], lhsT=st.vnew[:, h, :], rhs=qkt[:, h, :],
                        start=False, stop=True)), r=[k_(f"vnew{h // 4}"), qktk], w=[B0[1]])
                ob = st.ost[b]
                obk = f"ost{d}_{b}"
                S.add("act", (lambda e, ob=ob: e.activation(out=ob.rearrange("p h c -> p (h c)"),
                                                            in_=B0[0], func=AF.Identity)),
                      r=[B0[1]], w=[obk, B0[1]])
                l0 = (n - 4) * CH
                S.dma("sp", (lambda e, ob=ob, l0=l0: e.dma_start(out=o_d[d, :, :, l0:l0 + CH],
                                                                 in_=ob)), r=[obk], w=[f"o_d{d}_{n}"])
                yield
            S.add("pool", (lambda e, dS=dS: e.tensor_tensor(
                out=Sf, in0=Sf, in1=bcast(dS.unsqueeze(2), [128, H, DH]), op=ALU.mult)),
                r=[f"S32_{d}", dSk], w=[f"S32_{d}"])
            for hf in range(2):
                for h4 in range(4):
                    h = hf * 4 + h4
                    S.add("pe", (lambda e, h=h, h4=h4, kdec=kdec: e.matmul(
                        B0[0][:, h4 * DH:(h4 + 1) * DH], lhsT=kdec[:, h, :], rhs=st.vnew[:, h, :],
                        start=True, stop=True)), r=[kdeck, k_(f"vnew{hf}")], w=[B0[1]])
                S.add("dve", (lambda e, hf=hf: e.tensor_tensor(
                    out=Sf[:, hf * 4:hf * 4 + 4, :].rearrange("p h c -> p (h c)"), in0=B0[0],
                    in1=Sf[:, hf * 4:hf * 4 + 4, :].rearrange("p h c -> p (h c)"), op=ALU.add)),
                    r=[f"S32_{d}", B0[1]], w=[f"S32_{d}", B0[1]])
                yield
            S.add("act", (lambda e: e.activation(out=Sb.rearrange("p h c -> p (h c)"),
                                                 in_=Sf.rearrange("p h c -> p (h c)"),
                                                 func=AF.Identity)),
                  r=[f"S32_{d}"], w=[f"Sbf_{d}"])
            rstep[d] = step + 1
            yield

    mG = A.mark()
    sts = [mk_stream(0), mk_stream(1)]
    order_f = list(range(NCH))
    order_b = [3, 2, 1, 0] + list(range(NCH - 1, 3, -1))
    if stop_after.startswith("G") and len(stop_after) > 1:
        nst = int(stop_after[1:])
        order_f, order_b = order_f[:nst], order_b[:nst]
    pdone = [0, 0]
    rstep = [0, 0]
    gens = [g_prep(sts[0], order_f), g_recur(sts[0], order_f),
            g_prep(sts[1], order_b), g_recur(sts[1], order_b)]
    alive = [True] * 4
    while any(alive):
        for gi, g in enumerate(gens):
            if alive[gi]:
                try:
                    next(g)
                except StopIteration:
                    alive[gi] = False
    A.reset(mG)
    A.reset(persist_mark)
    phase_barrier()
    if stop_after.startswith("G"):
        return nc, S, tail

    waT = A.t([128, 8, D], BF16, "waT")
    wbT = A.t([128, 8, D], BF16, "wbT")
    woT = A.t([128, 8, D], BF16, "woT")
    for (dst, src, key) in [(waT, wa_d, "waT"), (wbT, wb_d, "wbT"), (woT, wo_d, "woT")]:
        S.dma("pool", (lambda e, dst=dst, src=src: e.dma_start(
            out=dst, in_=src.rearrange("(kc p) n -> p kc n", p=128))), w=[key])
    ones_b = A.t([128, 128], BF16, "ones_b")
    S.add("dve", lambda e: e.tensor_copy(out=ones_b, in_=ones), r=["consts"], w=["ones_b"])
    ofh = [A.t([128, 512], F32, f"ofh{i}") for i in range(2)]
    obh = [A.t([128, 512], F32, f"obh{i}") for i in range(2)]
    zah = [A.t([128, 512], F32, f"zah{i}") for i in range(2)]
    sqh = [A.t([128, 512], BF16, f"sqh{i}") for i in range(2)]
    rsh = [A.t([128, 512], F32, f"rsh{i}") for i in range(2)]
    yaT = [A.t([128, H, 512], BF16, f"yaT{i}") for i in range(2)]
    ybT = [A.t([128, 8, 512], BF16, f"ybT{i}") for i in range(2)]
    mgT = [A.t([128, 8, 512], BF16, f"mgT{i}") for i in range(2)]
    sgt = [A.t([128, 2, 512], F32, f"sgt{i}") for i in range(2)]
    mt = [A.t([128, 2, 512], F32, f"mt{i}") for i in range(2)]
    xt2 = [A.t([128, D], F32, f"xt2_{i}") for i in range(2)]
    zt = [A.t([128, D], F32, f"zt_{i}") for i in range(2)]
    st2 = [A.t([128, 16], F32, f"st2_{i}") for i in range(2)]
    gate_row, lng_row, lnb_row = rows[:, 0:D], rows[:, D:2 * D], rows[:, 2 * D:3 * D]

    def m_taskA(bl):
        l0 = bl * 512
        bb = bl % 2
        S.dma("sp", (lambda e: e.dma_start(out=ybT[bb], in_=yb_d[:, :, l0:l0 + 512])),
              r=[f"yb_d{j}_{bl + 1}" for j in range(8)], w=[f"ybT{bb}"])
        def head_loads(h):
            hb = h % 2
            S.dma("sp", (lambda e, h=h, hb=hb: e.dma_start(out=ofh[hb],
                                                           in_=o_d[0, :, h, l0:l0 + 512])),
                  r=[f"o_d0_{4 + bl * 8 + i}" for i in range(8)], w=[f"ofh{hb}"])
            S.dma("sp", (lambda e, h=h, hb=hb: e.dma_start(out=obh[hb],
                                                           in_=o_d[1, :, h, l0:l0 + 512])),
                  r=[f"o_d1_{4 + bl * 8 + i}" for i in range(8)], w=[f"obh{hb}"])
            S.dma("sp", (lambda e, h=h, hb=hb: e.dma_start(out=zah[hb],
                                                           in_=za_d[:, h, l0:l0 + 512])),
                  r=[f"za_d{h}_{bl + 1}"], w=[f"zah{hb}"])

        def sg_loads(m):
            b = m % 2
            S.dma("sp", (lambda e, b=b, m=m: e.dma_start(
                out=sgt[b][:, 0, :], in_=sg_d[:, m, l0:l0 + 512])),
                r=[f"sg_d{m}_{bl + 1}"], w=[f"sgt{b}a"])
            S.dma("sp", (lambda e, b=b, m=m: e.dma_start(
                out=sgt[b][:, 1, :], in_=sg_d[:, 8 + m, l0:l0 + 512])),
                r=[f"sg_d{8 + m}_{bl + 1}"], w=[f"sgt{b}b"])

        head_loads(0)
        for h in range(H):
            hb = h % 2
            if h + 1 < H:
                head_loads(h + 1)
            else:
                sg_loads(0)
            S.add("dve", (lambda e, hb=hb: e.tensor_tensor(out=ofh[hb], in0=ofh[hb], in1=obh[hb],
                                                           op=ALU.add)),
                  r=[f"ofh{hb}", f"obh{hb}"], w=[f"ofh{hb}"])
            S.add("pool", (lambda e, hb=hb: e.tensor_tensor(out=sqh[hb], in0=ofh[hb], in1=ofh[hb],
                                                            op=ALU.mult)),
                  r=[f"ofh{hb}"], w=[f"sqh{hb}"])
            yield
            pb, pbk = pbank(hb)
            S.add("pe", (lambda e, hb=hb, pb=pb: e.matmul(pb, lhsT=ones_b, rhs=sqh[hb], start=True,
                                                          stop=True)),
                  r=[f"sqh{hb}", "ones_b"], w=[pbk])
            yield
            S.add("act", (lambda e, hb=hb, pb=pb: e.activation(out=rsh[hb], in_=pb, func=AF.Ln,
                                                               bias=RMS_EPS_AP, scale=1.0 / DH)),
                  r=[pbk, "consts"], w=[f"rsh{hb}", pbk])
            S.add("act", (lambda e, hb=hb: e.activation(out=rsh[hb], in_=rsh[hb], func=AF.Exp,
                                                        scale=-0.5)),
                  r=[f"rsh{hb}"], w=[f"rsh{hb}"])
            yield
            S.add("dve", (lambda e, hb=hb: e.tensor_tensor(out=ofh[hb], in0=ofh[hb], in1=rsh[hb],
                                                           op=ALU.mult)),
                  r=[f"ofh{hb}", f"rsh{hb}"], w=[f"ofh{hb}"])
            S.add("dve", (lambda e, h=h, hb=hb: e.scalar_tensor_tensor(
                out=yaT[bb][:, h, :], in0=ofh[hb], scalar=onw[:, 0:1], in1=zah[hb],
                op0=ALU.mult, op1=ALU.mult)), r=[f"ofh{hb}", f"zah{hb}", "onw"], w=[f"yaT{bb}"])
            yield
        for m in range(8):
            b = m % 2
            if m + 1 < 8:
                sg_loads(m + 1)
            pa, pak = pbank(2 + b * 2)
            pbb, pbbk = pbank(3 + b * 2)
            for j in range(8):
                S.add("pe", (lambda e, j=j, m=m, pa=pa: e.matmul(
                    pa, lhsT=waT[:, j, m * 128:(m + 1) * 128], rhs=yaT[bb][:, j, :],
                    start=(j == 0), stop=(j == 7))), r=["waT", f"yaT{bb}"], w=[pak], c=300.0)
            for j in range(8):
                S.add("pe", (lambda e, j=j, m=m, pbb=pbb: e.matmul(
                    pbb, lhsT=wbT[:, j, m * 128:(m + 1) * 128], rhs=ybT[bb][:, j, :],
                    start=(j == 0), stop=(j == 7))), r=["wbT", f"ybT{bb}"], w=[pbbk], c=300.0)
            yield
            S.add("dve", (lambda e, b=b, pa=pa: e.tensor_tensor(
                out=mt[b][:, 0, :], in0=pa, in1=sgt[b][:, 0, :], op=ALU.mult)),
                r=[pak, f"sgt{b}a"], w=[f"mt{b}a", pak])
            S.add("dve", (lambda e, b=b, pbb=pbb: e.tensor_tensor(
                out=mt[b][:, 1, :], in0=pbb, in1=sgt[b][:, 1, :], op=ALU.mult)),
                r=[pbbk, f"sgt{b}b"], w=[f"mt{b}b", pbbk])
            S.add("pool", (lambda e, b=b, m=m: e.tensor_tensor(
                out=mgT[bb][:, m, :], in0=mt[b][:, 0, :], in1=mt[b][:, 1, :], op=ALU.add)),
                r=[f"mt{b}a", f"mt{b}b"], w=[f"mgT{bb}"])
            yield

    mcnt = [0]
    pend_store = []

    def m_taskB(bl):
        l0 = bl * 512
        bb = bl % 2
        PY = PT[3]
        def x_load(sub, b):
            r0 = l0 + sub * 128
            S.dma("sp", (lambda e, b=b, r0=r0: e.dma_start(out=xt2[b], in_=x_d[r0:r0 + 128, :])),
                  w=[f"xt2_{b}"])

        def flush_store():
            if pend_store:
                b_, r0_ = pend_store.pop()
                op = S.dma("sp", (lambda e: e.dma_start(out=out_d[r0_:r0_ + 128, :], in_=xt2[b_])),
                           r=[f"xt2_{b_}"], w=["out_d"])
                tail.append(op)

        flush_store()
        x_load(0, mcnt[0] % 2)
        for sub in range(4):
            b = mcnt[0] % 2
            mcnt[0] += 1
            r0 = l0 + sub * 128
            flush_store()
            if sub + 1 < 4:
                x_load(sub + 1, mcnt[0] % 2)
            for half in range(2):
                pbk = f"ps{6 + half}"
                for m in range(8):
                    S.add("pe", (lambda e, m=m, half=half, sub=sub: e.matmul(
                        PY[:, half * 512:(half + 1) * 512],
                        lhsT=mgT[bb][:, m, sub * 128:(sub + 1) * 128],
                        rhs=woT[:, m, half * 512:(half + 1) * 512],
                        start=(m == 0), stop=(m == 7))), r=[f"mgT{bb}", "woT"], w=[pbk], c=300.0)
            yield
            S.add("dve", (lambda e, b=b: e.tensor_tensor(out=zt[b], in0=PY, in1=gate_row,
                                                         op=ALU.mult)),
                  r=["ps6", "ps7", "rows"], w=[f"zt_{b}", "ps6", "ps7"])
            S.add("dve", (lambda e, b=b: e.scalar_tensor_tensor(
                out=zt[b], in0=xt2[b], scalar=float(DN_ALPHA), in1=zt[b], op0=ALU.mult,
                op1=ALU.add)), r=[f"xt2_{b}", f"zt_{b}"], w=[f"zt_{b}"])
            yield
            st = st2[b]
            kst = f"st2_{b}"
            S.add("dve", (lambda e, b=b, st=st: e.bn_stats(out=st[:, 0:6], in_=zt[b][:, 0:512])),
                  r=[f"zt_{b}"], w=[kst + "a"])
            S.add("dve", (lambda e, b=b, st=st: e.bn_stats(out=st[:, 6:12], in_=zt[b][:, 512:1024])),
                  r=[f"zt_{b}"], w=[kst + "b"])
            S.add("dve", (lambda e, st=st: e.bn_aggr(out=st[:, 12:14], in_=st[:, 0:12])),
                  r=[kst + "a", kst + "b"], w=[kst + "mv"])
            S.add("act", (lambda e, st=st: e.activation(out=st[:, 14:15], in_=st[:, 13:14],
                                                        func=AF.Ln, bias=LN_EPS_AP, scale=1.0)),
                  r=[kst + "mv", "consts"], w=[kst + "r0"])
            S.add("act", (lambda e, st=st: e.activation(out=st[:, 14:15], in_=st[:, 14:15],
                                                        func=AF.Exp, scale=-0.5)),
                  r=[kst + "r0"], w=[kst + "r"])
            S.add("dve", (lambda e, st=st: e.scalar_tensor_tensor(
                out=st[:, 15:16], in0=st[:, 12:13], scalar=-1.0, in1=st[:, 14:15],
                op0=ALU.mult, op1=ALU.mult)), r=[kst + "mv", kst + "r"], w=[kst + "n"])
            yield
            S.add("act", (lambda e, b=b, st=st: e.activation(
                out=zt[b], in_=zt[b], func=AF.Identity, bias=st[:, 15:16], scale=st[:, 14:15])),
                r=[f"zt_{b}", kst + "r", kst + "n"], w=[f"zt_{b}"])
            yield
            S.add("pool", (lambda e, b=b: e.tensor_tensor(out=zt[b], in0=zt[b], in1=lng_row,
                                                          op=ALU.mult)),
                  r=[f"zt_{b}", "rows"], w=[f"zt_{b}"])
            S.add("pool", (lambda e, b=b: e.tensor_tensor(out=xt2[b], in0=zt[b], in1=lnb_row,
                                                          op=ALU.add)),
                  r=[f"zt_{b}", "rows"], w=[f"xt2_{b}"])
            pend_store.append((b, r0))
            yield

    def run_together(gens):
        alive = list(gens)
        while alive:
            for g in list(alive):
                try:
                    next(g)
                except StopIteration:
                    alive.remove(g)

    run_together([m_taskA(0)])
    for bl in range(1, 8):
        run_together([m_taskA(bl), m_taskB(bl - 1)])
    run_together([m_taskB(7)])
    if pend_store:
        b_, r0_ = pend_store.pop()
        tail.append(S.dma("sp", (lambda e: e.dma_start(out=out_d[r0_:r0_ + 128, :], in_=xt2[b_])),
                          r=[f"xt2_{b_}"], w=["out_d"]))
    return nc, S, tail


def _host_consts():
    c = np.zeros((128, 1024), np.float32)
    c[:, 0:128] = np.eye(128, dtype=np.float32)
    c[:, 128:256] = 1.0
    k = np.arange(64)[:, None]
    cc = np.arange(64)[None, :]
    c[0:64, 256:320] = (k <= cc)
    c[0:64, 320:384] = (k >= cc)
    r = np.arange(64)[:, None]
    q = np.arange(64)[None, :]
    m = [[(q < r), (q >= r)],
         [(q > r), (q <= r)]]
    for d in range(2):
        for w in range(2):
            o = 384 + (d * 2 + w) * 64
            c[0:64, o:o + 64] = np.where(m[d][w], 0.0, NEG)
    c[:, 640] = LN_EPS
    c[:, 641] = L2_EPS
    c[:, 642] = RMS_EPS
    c[:, 643] = 1.0
    return c


_CACHE = {}


def _get_program(debug=False, stop_after="M"):
    key = (debug, stop_after)
    if key not in _CACHE:
        nc, S, tail = build(debug, stop_after)
        S.finalize()
        S.emit(tail_ops=tail)
        _CACHE[key] = (nc, S)
    return _CACHE[key]


def _prep_inputs(x, c, ctx, c_ctx, w_mod, b_mod, w_in, b_in, conv_qkv_w, a_log, dt_bias,
                 o_norm_w, conv_b_w, conv_b_b, w_a, w_b, w_out, ln_g, ln_b):
    f = lambda a: np.ascontiguousarray(np.asarray(a, dtype=np.float32))
    x, c, ctx, c_ctx = f(x), f(c), f(ctx), f(c_ctx)
    w_mod, b_mod, w_in, b_in = f(w_mod)[0], f(b_mod)[0], f(w_in)[0], f(b_in)[0]
    cq, al, dtb = f(conv_qkv_w)[0], f(a_log)[0], f(dt_bias)[0]
    onw, cbw, cbb = f(o_norm_w)[0], f(conv_b_w)[0], f(conv_b_b)[0]
    w_a, w_b, w_out, ln_g, ln_b = f(w_a)[0], f(w_b)[0], f(w_out)[0], f(ln_g)[0], f(ln_b)[0]
    rep = lambda v: np.ascontiguousarray(np.broadcast_to(v[None, :], (128, v.shape[0])))
    bmod_fm = np.ascontiguousarray(b_mod.reshape(24, 128).T)
    rows = np.concatenate([rep(b_mod[2048:3072]), rep(ln_g), rep(ln_b)], axis=1)
    cols = np.concatenate([np.arange(0, 4096), np.arange(4128, IN_COLS)])
    bin_fm = np.ascontiguousarray(b_in[cols].reshape(80, 128).T)
    small = np.concatenate([rep(b_in[4096:4128]), rep(al.reshape(-1)), rep(dtb.reshape(-1))], axis=1)
    cqkv_fm = np.ascontiguousarray(cq.reshape(3, 24, 128).transpose(2, 1, 0).reshape(128, 72))
    cb4 = np.concatenate([cbw, cbb[None, :]], axis=0)
    cb_fm = np.ascontiguousarray(cb4.reshape(4, 8, 128).transpose(2, 1, 0).reshape(128, 32))
    common = {
        "w_mod": w_mod, "bmod_fm": bmod_fm, "rows": np.ascontiguousarray(rows), "w_in": w_in,
        "bin_fm": bin_fm, "small_rows": np.ascontiguousarray(small), "cqkv_fm": cqkv_fm,
        "cb_fm": cb_fm, "onw": np.ascontiguousarray(onw.reshape(128, 1)), "w_a": w_a, "w_b": w_b,
        "w_out": w_out, "consts": _host_consts(),
    }
    maps = []
    for b in range(x.shape[0]):
        cv = np.stack([c[b].reshape(8, 128).T, c_ctx.reshape(8, 128).T], axis=2)
        m = dict(common)
        m["x"] = x[b]
        m["ctx"] = ctx[b]
        m["cvec"] = np.ascontiguousarray(cv.reshape(128, 16))
        maps.append(m)
    return maps


def kernel(**inputs):
    maps = _prep_inputs(**inputs)
    nc, S = _get_program(False, "M")
    res = run_bass_kernel_spmd(nc, maps, core_ids=list(range(8)))
    return np.stack([np.asarray(r["out"], dtype=np.float32) for r in res.results], axis=0)
```
